# Optimizing a Trainium2 kernel written in Bass

```python
import math
import jax, jax.numpy as jnp
from jax import lax
import numpy as np

D_MODEL = 2048
BATCH = 2
SEQ = 4096
DEPTH = 4
DEC_BATCH = 8
DEC_SEQ = 8
PAST_LEN = 16384
PAGE_SIZE = 128

D_MIX = D_MODEL
HEAD_DIM_A = 64
D_A = (3 * D_MIX) // 8
H_A = D_A // HEAD_DIM_A
D_B = D_MIX // 4
G_B = 4
C_B = D_B // G_B
CHUNK_B = 128
D_C = D_MIX - D_A - D_B
DH_C = 128
H_C = D_C // DH_C
CHUNK_C = 128
CONV_W = 4
PATTERNS = ((128, 1), (512, 4), (2048, 16))
WIN_MAX = 2048
BLK_A = 128
N_BUCKETS = 32
MAX_DIST = 2048
EPS = 1e-6

_SIZES = (D_A, D_A, D_A, D_A, D_B, D_B, D_B, 2 * D_C, D_C, D_C, D_C, H_C, H_C)
D_IN = sum(_SIZES)
SPLIT_IDX = tuple(int(s) for s in np.cumsum(_SIZES)[:-1])

kernel_name = 'hybrid_dilated_gmlp_mlstm_decoder_step'


def _rmsnorm(x, g):
    xf = x.astype(jnp.float32)
    y = xf * lax.rsqrt(jnp.mean(xf * xf, axis=-1, keepdims=True) + EPS)
    return (y * g.astype(jnp.float32)).astype(x.dtype)


def _t5_bucket(dist):
    max_exact = N_BUCKETS // 2
    df = jnp.maximum(dist, 1).astype(jnp.float32)
    large = max_exact + (jnp.log(df / max_exact) / math.log(MAX_DIST / max_exact)
                         * (N_BUCKETS - max_exact)).astype(jnp.int32)
    return jnp.where(dist < max_exact, dist, jnp.minimum(large, N_BUCKETS - 1))


def _combine(outs, lses):
    w = jax.nn.softmax(jnp.stack(lses), axis=0)
    return jnp.einsum('pbsh,pbshd->bshd', w, jnp.stack(outs))


def _dilated_attn_prompt(q, k, v, rel_bias):
    B, S, H, Dh = q.shape
    outs, lses = [], []
    for win, dil in PATTERNS:
        n_back = win // dil
        blk = BLK_A
        nb = -(-S // (dil * blk))
        sp = nb * blk * dil
        padw = ((0, 0), (0, sp - S), (0, 0), (0, 0))
        qs = jnp.pad(q, padw).reshape(B, nb, blk, dil, H, Dh)
        ks = jnp.pad(k, padw).reshape(B, nb, blk, dil, H, Dh)
        vs = jnp.pad(v, padw).reshape(B, nb, blk, dil, H, Dh)
        prev = ((0, 0), (1, 0), (0, 0), (0, 0), (0, 0), (0, 0))
        kk = jnp.concatenate([jnp.pad(ks, prev)[:, :-1], ks], axis=2)
        vv = jnp.concatenate([jnp.pad(vs, prev)[:, :-1], vs], axis=2)
        qi = jnp.arange(blk)[:, None]
        ki = jnp.arange(2 * blk)[None, :]
        j = qi + blk - ki
        band = (j >= 0) & (j <= n_back)
        first = (jnp.arange(nb) == 0)[:, None, None] & (ki < blk)[None]
        mask = band[None] & ~first
        bias = rel_bias[_t5_bucket(jnp.clip(j, 0, n_back) * dil)].astype(jnp.float32).transpose(2, 0, 1)
        s = jnp.einsum('bnqrhd,bnkrhd->bnrhqk', qs, kk).astype(jnp.float32) + bias
        s = jnp.where(mask[None, :, None, None], s, -jnp.inf)
        lse = jax.nn.logsumexp(s, axis=-1)
        p = jnp.exp(s - lse[..., None])
        o = jnp.einsum('bnrhqk,bnkrhd->bnqrhd', p, vv.astype(jnp.float32))
        outs.append(o.reshape(B, sp, H, Dh)[:, :S])
        lses.append(lse.transpose(0, 1, 4, 2, 3).reshape(B, sp, H)[:, :S])
    return _combine(outs, lses)


def _dilated_attn_sample(q, k_all, v_all, rel_bias, n_past):
    T = q.shape[1]
    outs, lses = [], []
    for win, dil in PATTERNS:
        jj = jnp.arange(win // dil + 1)
        idx = n_past + jnp.arange(T)[:, None] - jj[None, :] * dil
        valid = idx >= 0
        idx = jnp.maximum(idx, 0)
        kg = k_all[:, idx]
        vg = v_all[:, idx]
        bias = rel_bias[_t5_bucket(jj * dil)].astype(jnp.float32).T
        s = jnp.einsum('bthd,btjhd->bthj', q, kg).astype(jnp.float32) + bias[None, None]
        s = jnp.where(valid[None, :, None, :], s, -jnp.inf)
        lse = jax.nn.logsumexp(s, axis=-1)
        p = jnp.exp(s - lse[..., None])
        outs.append(jnp.einsum('bthj,btjhd->bthd', p, vg.astype(jnp.float32)))
        lses.append(lse)
    return _combine(outs, lses)


def _sgu(u, vn, w_s, b_s):
    B, S, _ = u.shape
    L = min(CHUNK_B, S)
    nc = S // L
    w = jnp.tril(w_s[:, :L, :L])
    vg = vn.reshape(B, nc, L, G_B, C_B)
    mix = jnp.einsum('gts,bnsgc->bntgc', w, vg) + b_s[:, :L].T[None, None, :, :, None]
    return u * mix.reshape(B, S, D_B).astype(u.dtype)


def _mlstm(q, k, v, i_pre, f_pre, C0, n0, m0):
    B, S, H, D = q.shape
    L = min(CHUNK_C, S)
    nc = S // L
    f32 = jnp.float32
    q = q.astype(f32)
    k = k.astype(f32) * (D ** -0.5)
    v = v.astype(f32)
    ig = i_pre.astype(f32)
    logf = jax.nn.log_sigmoid(f_pre.astype(f32))

    def chunks(a):
        return a.reshape((B, nc, L) + a.shape[2:]).swapaxes(0, 1)

    causal = jnp.tril(jnp.ones((L, L), dtype=bool))

    def step(carry, inp):
        C, n, m = carry
        qc, kc, vc, ic, fc = inp
        bh = jnp.cumsum(fc, axis=1).transpose(0, 2, 1)
        ih = ic.transpose(0, 2, 1)
        dlog = jnp.where(causal, bh[:, :, :, None] - bh[:, :, None, :] + ih[:, :, None, :], -jnp.inf)
        inter = bh + m[:, :, None]
        mt = jnp.maximum(inter, jnp.max(dlog, axis=-1))
        a = jnp.exp(dlog - mt[..., None]) * jnp.einsum('bthd,bshd->bhts', qc, kc)
        w_inter = jnp.exp(inter - mt)
        num = (jnp.einsum('bhts,bshd->bthd', a, vc)
               + w_inter.transpose(0, 2, 1)[..., None] * jnp.einsum('bhkv,bthk->bthv', C, qc))
        den = jnp.sum(a, axis=-1) + w_inter * jnp.einsum('bhk,bthk->bht', n, qc)
        denom = jnp.maximum(jnp.abs(den), jnp.exp(-mt))
        h = num / denom.transpose(0, 2, 1)[..., None]
        bl = bh[:, :, -1]
        g = bl[:, :, None] - bh + ih
        m_new = jnp.maximum(bl + m, jnp.max(g, axis=-1))
        ws = jnp.exp(g - m_new[..., None])
        wc = jnp.exp(bl + m - m_new)
        C_new = wc[..., None, None] * C + jnp.einsum('bhs,bshk,bshv->bhkv', ws, kc, vc)
        n_new = wc[..., None] * n + jnp.einsum('bhs,bshk->bhk', ws, kc)
        return (C_new, n_new, m_new), h

    init = (C0.astype(f32), n0.astype(f32), m0.astype(f32))
    (Cf, nf, mf), hs = lax.scan(step, init, (chunks(q), chunks(k), chunks(v), chunks(ig), chunks(logf)))
    return hs.swapaxes(0, 1).reshape(B, S, H, D), Cf, nf, mf


def _layer(x, c, norm_g, ada_w, ada_b, w_in, qn_g, kn_g, sgu_g, sgu_w, sgu_b, conv_w, conv_b,
           f_bias, i_bias, hn_g, w_out, rel_bias,
           k_buf=None, v_buf=None, conv_buf=None, C0=None, n0=None, m0=None):
    B, S, _ = x.shape
    dt = x.dtype
    f32 = jnp.float32
    mod = jax.nn.silu(c.astype(f32)) @ ada_w.astype(f32) + ada_b.astype(f32)
    shift, scale, gate = jnp.split(mod, 3, axis=-1)
    h = (_rmsnorm(x, norm_g).astype(f32) * (1.0 + scale[:, None]) + shift[:, None]).astype(dt)
    proj = h @ w_in
    (q_a, k_a, v_a, z_a, u_b, v_b, z_b, qk_c, v_c, o_c, z_c, i_c, f_c) = jnp.split(proj, SPLIT_IDX, axis=-1)

    q = _rmsnorm(q_a.reshape(B, S, H_A, HEAD_DIM_A), qn_g) * (HEAD_DIM_A ** -0.5)
    k = _rmsnorm(k_a.reshape(B, S, H_A, HEAD_DIM_A), kn_g)
    v = v_a.reshape(B, S, H_A, HEAD_DIM_A)
    if k_buf is None:
        attn = _dilated_attn_prompt(q, k, v, rel_bias)
        wkeep = min(WIN_MAX, S)
        new_k, new_v = k[:, S - wkeep:], v[:, S - wkeep:]
    else:
        k_all = jnp.concatenate([k_buf.astype(k.dtype), k], axis=1)
        v_all = jnp.concatenate([v_buf.astype(v.dtype), v], axis=1)
        attn = _dilated_attn_sample(q, k_all, v_all, rel_bias, k_buf.shape[1])
        new_k, new_v = k, v
    y_a = attn.reshape(B, S, D_A).astype(dt) * jax.nn.silu(z_a)

    vn = _rmsnorm(v_b, sgu_g)
    y_b = _sgu(u_b, vn, sgu_w, sgu_b) * jax.nn.silu(z_b)

    if conv_buf is None:
        conv_buf = jnp.zeros((B, CONV_W - 1, 2 * D_C), dt)
    xp = jnp.concatenate([conv_buf.astype(qk_c.dtype), qk_c], axis=1)
    qk = conv_b
    for j in range(CONV_W):
        qk = qk + conv_w[j] * xp[:, j:j + S]
    q_c, k_c = jnp.split(jax.nn.silu(qk), 2, axis=-1)
    new_conv = xp[:, -(CONV_W - 1):]
    if C0 is None:
        C0 = jnp.zeros((B, H_C, DH_C, DH_C), f32)
        n0 = jnp.zeros((B, H_C, DH_C), f32)
        m0 = jnp.zeros((B, H_C), f32)
    hc, Cn, nn_, mn = _mlstm(q_c.reshape(B, S, H_C, DH_C), k_c.reshape(B, S, H_C, DH_C),
                             v_c.reshape(B, S, H_C, DH_C), i_c + i_bias, f_c + f_bias, C0, n0, m0)
    hc = _rmsnorm(hc, hn_g.reshape(H_C, DH_C))
    y_c = (hc.reshape(B, S, D_C) * jax.nn.sigmoid(o_c.astype(f32)) * jax.nn.silu(z_c.astype(f32))).astype(dt)

    y = jnp.concatenate([y_a, y_b, y_c], axis=-1) @ w_out
    x = x + (gate[:, None] * y.astype(f32)).astype(dt)
    return x, new_k, new_v, vn, new_conv, Cn, nn_, mn


def setup_inputs(seed: int = 0) -> dict:
    key = jax.random.key(seed)
    ks = jax.random.split(key, 26)
    f32 = jnp.float32

    def nrm(k, shape, s):
        return jax.random.normal(k, shape, f32) * s

    win_buf = min(WIN_MAX, PAST_LEN)
    ada_b = nrm(ks[13], (DEPTH, 3 * D_MODEL), 0.02).at[:, 2 * D_MODEL:].add(1.0)
    return {
        'x_prompt': nrm(ks[0], (BATCH, SEQ, D_MODEL), 1.0),
        'x_sample': nrm(ks[1], (DEC_BATCH, DEC_SEQ, D_MODEL), 1.0),
        'c_prompt': nrm(ks[2], (BATCH, D_MODEL), 1.0),
        'c_sample': nrm(ks[3], (DEC_BATCH, D_MODEL), 1.0),
        'cache_k_win': nrm(ks[4], (DEPTH, DEC_BATCH, win_buf, H_A, HEAD_DIM_A), 1.0),
        'cache_v_win': nrm(ks[5], (DEPTH, DEC_BATCH, win_buf, H_A, HEAD_DIM_A), 1.0),
        'state_conv': nrm(ks[6], (DEPTH, DEC_BATCH, CONV_W - 1, 2 * D_C), 1.0),
        'state_C': nrm(ks[7], (DEPTH, DEC_BATCH, H_C, DH_C, DH_C), 0.1),
        'state_n': nrm(ks[8], (DEPTH, DEC_BATCH, H_C, DH_C), 0.1),
        'state_m': nrm(ks[9], (DEPTH, DEC_BATCH, H_C), 1.0),
        'rel_bias': nrm(ks[10], (N_BUCKETS, H_A), 0.2),
        'norm_g': 1.0 + nrm(ks[11], (DEPTH, D_MODEL), 0.02),
        'ada_w': nrm(ks[12], (DEPTH, D_MODEL, 3 * D_MODEL), 0.1 * D_MODEL ** -0.5),
        'ada_b': ada_b,
        'w_in': nrm(ks[14], (DEPTH, D_MODEL, D_IN), D_MODEL ** -0.5),
        'qn_g': 1.0 + nrm(ks[15], (DEPTH, HEAD_DIM_A), 0.02),
        'kn_g': 1.0 + nrm(ks[16], (DEPTH, HEAD_DIM_A), 0.02),
        'sgu_g': 1.0 + nrm(ks[17], (DEPTH, D_B), 0.02),
        'sgu_w': nrm(ks[18], (DEPTH, G_B, CHUNK_B, CHUNK_B), CHUNK_B ** -0.5),
        'sgu_b': 1.0 + nrm(ks[19], (DEPTH, G_B, CHUNK_B), 0.02),
        'conv_w': nrm(ks[20], (DEPTH, CONV_W, 2 * D_C), CONV_W ** -0.5),
        'conv_b': nrm(ks[21], (DEPTH, 2 * D_C), 0.02),
        'f_bias': 3.0 + 3.0 * jax.random.uniform(ks[22], (DEPTH, H_C), f32),
        'i_bias': nrm(ks[23], (DEPTH, H_C), 0.1),
        'hn_g': 1.0 + nrm(ks[24], (DEPTH, D_C), 0.02),
        'w_out': nrm(ks[25], (DEPTH, D_MIX, D_MODEL), D_MIX ** -0.5),
    }


def reference(x_prompt, x_sample, c_prompt, c_sample, cache_k_win, cache_v_win, state_conv, state_C,
              state_n, state_m, rel_bias, norm_g, ada_w, ada_b, w_in, qn_g, kn_g, sgu_g, sgu_w, sgu_b,
              conv_w, conv_b, f_bias, i_bias, hn_g, w_out):
    xp, xs = x_prompt, x_sample
    p_k, p_v, p_conv, p_C, p_n, p_m = [], [], [], [], [], []
    s_k, s_v, s_sgu, s_conv, s_C, s_n, s_m = [], [], [], [], [], [], []
    for l in range(DEPTH):
        w = (norm_g[l], ada_w[l], ada_b[l], w_in[l], qn_g[l], kn_g[l], sgu_g[l], sgu_w[l], sgu_b[l],
             conv_w[l], conv_b[l], f_bias[l], i_bias[l], hn_g[l], w_out[l], rel_bias)
        xp, nk, nv, _, ncv, nC, nn_, nm = _layer(xp, c_prompt, *w)
        p_k.append(nk); p_v.append(nv); p_conv.append(ncv); p_C.append(nC); p_n.append(nn_); p_m.append(nm)
        xs, nk, nv, nsv, ncv, nC, nn_, nm = _layer(xs, c_sample, *w, cache_k_win[l], cache_v_win[l],
                                                  state_conv[l], state_C[l], state_n[l], state_m[l])
        s_k.append(nk); s_v.append(nv); s_sgu.append(nsv); s_conv.append(ncv)
        s_C.append(nC); s_n.append(nn_); s_m.append(nm)
    return (xp, xs,
            jnp.stack(p_k), jnp.stack(p_v), jnp.stack(p_conv), jnp.stack(p_C), jnp.stack(p_n), jnp.stack(p_m),
            jnp.stack(s_k), jnp.stack(s_v), jnp.stack(s_sgu), jnp.stack(s_conv), jnp.stack(s_C),
            jnp.stack(s_n), jnp.stack(s_m))
```

```python
import math
import numpy as np
import ml_dtypes
import concourse.bass as bass
import concourse.mybir as mybir
from concourse.bass_utils import run_bass_kernel_spmd

F32 = mybir.dt.float32
BF16 = mybir.dt.bfloat16
AF = mybir.ActivationFunctionType
ALU = mybir.AluOpType
AX = mybir.AxisListType

D = 2048
KC = 16
D_A, D_B, D_C = 768, 512, 768
H_A, H_C = 12, 6
D_IN = 8460
EPS = 1e-6
OFF = dict(q_a=0, k_a=768, v_a=1536, z_a=2304, u_b=3072, v_b=3584, z_b=4096, qk_c=4608,
           v_c=6144, o_c=6912, z_c=7680, i_c=8448, f_c=8454)
NZ = 17 * 128 + 256


class _Stop(Exception):
    pass


class Cfg:
    stage = 99

    def __init__(self, depth=4, seq=4096, nsamp_tok=8, win=2048):
        self.depth, self.seq, self.T, self.win = depth, seq, nsamp_tok, win
        self.nsup = seq // 1024
        self.wkeep = min(2048, seq)


class KB:
    def __init__(self, nc):
        self.nc = nc
        self.E = dict(pe=nc.tensor, act=nc.scalar, dve=nc.vector, pool=nc.gpsimd, sp=nc.sync)
        self.sems = {}
        self.cnt = {}
        self.waited = {}
        self.res = {}
        self.ctx = []
        self.semctx = []
        for e in self.E:
            self._sem("E_" + e)

    def _sem(self, key):
        if key not in self.sems:
            cm = self.nc.semaphore("s%d" % len(self.sems))
            self.sems[key] = cm.__enter__()
            self.semctx.append(cm)
            self.cnt[key] = 0
        return self.sems[key]

    def alloc(self, name, shape, dt, psum=False):
        self.nalloc = getattr(self, "nalloc", 0) + 1
        cm = (self.nc.psum_tensor if psum else self.nc.sbuf_tensor)("%s_%d" % (name, self.nalloc), list(shape), dt)
        t = cm.__enter__()
        self.ctx.append(cm)
        return t

    def _wait(self, eng, deps):
        for (skey, val) in deps:
            if skey == "E_" + eng and eng == "pe":
                continue
            if skey.startswith("D_"):
                val = max(val, self.cnt[skey])
            k = (eng, skey)
            if self.waited.get(k, 0) >= val:
                continue
            self.E[eng].wait_ge(self.sems[skey], val)
            self.waited[k] = val

    def _deps(self, reads, writes):
        deps = {}
        for k in reads:
            r = self.res.get(k)
            if r and r["w"]:
                s, v = r["w"]
                deps[s] = max(deps.get(s, 0), v)
        for k in writes:
            r = self.res.get(k)
            if r:
                if r["w"]:
                    s, v = r["w"]
                    deps[s] = max(deps.get(s, 0), v)
                for s, v in r["r"].items():
                    deps[s] = max(deps.get(s, 0), v)
        return list(deps.items())

    def _commit(self, tok, reads, writes):
        for k in writes:
            self.res[k] = {"w": tok, "r": {}}
        for k in reads:
            r = self.res.setdefault(k, {"w": None, "r": {}})
            r["r"][tok[0]] = max(r["r"].get(tok[0], 0), tok[1])

    def op(self, eng, fn, reads=(), writes=()):
        self._wait(eng, self._deps(reads, writes))
        ins = fn(self.E[eng])
        skey = "E_" + eng
        ins.then_inc(self.sems[skey], 1)
        self.cnt[skey] += 1
        self._commit((skey, self.cnt[skey]), reads, writes)

    def dma(self, q, stream, out, in_, reads=(), writes=(), **kw):
        skey = "D_" + stream
        self._sem(skey)
        deps = [d for d in self._deps(reads, writes) if d[0] != skey]
        self._wait(q, deps)
        ins = self.E[q].dma_start(out=out, in_=in_, **kw)
        ins.then_inc(self.sems[skey], 16)
        self.cnt[skey] += 16
        self._commit((skey, self.cnt[skey]), reads, writes)

    def mark(self):
        return len(self.ctx)

    def barrier(self):
        deps = [(s, c) for s, c in self.cnt.items() if c > 0]
        for e in self.E:
            self._wait(e, deps)

    def release(self, m):
        self.barrier()
        while len(self.ctx) > m:
            self.ctx.pop().__exit__(None, None, None)

    def finish(self, eng="sp"):
        deps = [(s, c) for s, c in self.cnt.items() if s.startswith("D_") and c > 0]
        deps += [(s, c) for s, c in self.cnt.items() if s.startswith("E_") and c > 0]
        self._wait(eng, deps)


def host_consts():
    c = {}
    c["ident"] = np.eye(128, dtype=np.float32)
    c["antiid"] = np.ascontiguousarray(np.eye(128, dtype=np.float32)[::-1])
    bo = np.zeros((128, 128), np.float32)
    bo[:64, :64] = 1.0 / 64
    bo[64:, 64:] = 1.0 / 64
    c["blockones"] = bo
    c["triu"] = np.triu(np.ones((128, 128), np.float32))
    sel = np.zeros((6, 6, 128), np.float32)
    for h in range(6):
        sel[h, h, :] = 1.0
    c["sel6"] = sel.reshape(6, 768)
    oh = np.zeros((35, NZ), np.float32)
    for x in range(NZ):
        d = x - 127
        mult = 0
        if 0 <= d <= 2048:
            mult = int(d <= 128) + int(d % 4 == 0 and d <= 512) + int(d % 16 == 0)
        if mult == 0:
            oh[32, x] = 1.0
            continue
        if d < 16:
            b = d
        else:
            b = min(16 + int(np.float32(np.log(np.float32(d) / np.float32(16)) / np.float32(math.log(2048 / 16)) * np.float32(16))), 31)
        oh[b, x] = 1.0
        if mult == 2:
            oh[33, x] = 1.0
        if mult == 3:
            oh[34, x] = 1.0
    c["ohz"] = oh
    tail = np.zeros((3, 12), np.float32)
    tail[0] = -30000.0
    tail[1] = math.log(2.0)
    tail[2] = math.log(3.0)
    c["relb_tail"] = tail
    return c


def _bucket_check():
    import jax.numpy as jnp
    return True


def build_program(cfg):
    nc = bass.Bass("TRN2", target_bir_lowering=False)
    kb = KB(nc)
    L, SEQ, T, NSUP = cfg.depth, cfg.seq, cfg.T, cfg.nsup
    WK = cfg.wkeep
    WIN = cfg.win

    def din(name, shape, dt=F32):
        return nc.dram_tensor(name, list(shape), dt, kind="ExternalInput").ap()

    def dout(name, shape):
        return nc.dram_tensor(name, list(shape), F32, kind="ExternalOutput").ap()

    def dscr(name, shape, dt):
        return nc.dram_tensor(name, list(shape), dt, kind="Internal").ap()

    x_p = din("x_p", [SEQ, D]); x_s = din("x_s", [T, D])
    c_in = din("c_in", [2, D])
    ck = din("ck", [L, WIN, 768]); cv = din("cv", [L, WIN, 768])
    st_conv = din("st_conv", [L, 3, 1536]); st_C = din("st_C", [L, 6, 128, 128])
    st_n = din("st_n", [L, 6, 128]); st_m = din("st_m", [L, 6])
    rel_bias = din("rel_bias", [32, 12]); norm_g = din("norm_g", [L, D])
    ada_w = din("ada_w", [L, D, 3 * D]); ada_b = din("ada_b", [L, 3 * D])
    w_in = din("w_in", [L, D, D_IN]); qn_g = din("qn_g", [L, 64]); kn_g = din("kn_g", [L, 64])
    sgu_g = din("sgu_g", [L, 512]); sgu_w = din("sgu_w", [L, 4, 128, 128]); sgu_b = din("sgu_b", [L, 4, 128])
    conv_w = din("conv_w", [L, 4, 1536]); conv_b = din("conv_b", [L, 1536])
    f_bias = din("f_bias", [L, 6]); i_bias = din("i_bias", [L, 6]); hn_g = din("hn_g", [L, 768])
    w_out = din("w_out", [L, D, D])
    hc = {k: din("c_" + k, list(v.shape)) for k, v in host_consts().items()}

    y_p = dout("y_p", [SEQ, D]); y_s = dout("y_s", [T, D])
    p_k = dout("p_k", [L, WK, 768]); p_v = dout("p_v", [L, WK, 768])
    p_conv = dout("p_conv", [L, 3, 1536]); p_C = dout("p_C", [L, 6, 128, 128])
    p_n = dout("p_n", [L, 6, 128]); p_m = dout("p_m", [L, 6])
    s_k = dout("s_k", [L, T, 768]); s_v = dout("s_v", [L, T, 768]); s_sgu = dout("s_sgu", [L, T, 512])
    s_conv = dout("s_conv", [L, 3, 1536]); s_C = dout("s_C", [L, 6, 128, 128])
    s_n = dout("s_n", [L, 6, 128]); s_m = dout("s_m", [L, 6])

    modd = dscr("modd", [2, 3 * D], F32)
    zt = dscr("zt", [12, NZ], F32)
    ebd = dscr("ebd", [12, 128, 17 * 128], BF16)
    ktd = dscr("ktd", [768, SEQ + 8], BF16)
    vsd = dscr("vsd", [SEQ + 8, 768], BF16)
    xs_res = dscr("xs_res", [T, D], F32)

    A = kb.alloc
    ident = A("ident", [128, 128], F32); antiid = A("antiid", [128, 128], F32)
    blockones_f = A("blockones_f", [128, 128], F32); triu = A("triu", [128, 128], F32)
    ident_b = A("ident_b", [128, 128], BF16); blockones = A("blockones", [128, 128], BF16)
    ones_b = A("ones_b", [128, 128], BF16); o128_b = A("o128_b", [128, 128], BF16)
    triu_b = A("triu_b", [128, 128], BF16)
    sel6 = A("sel6", [6, 768], F32)
    ones_f = A("ones_f", [128, 128], F32)
    for nm, t in (("ident", ident), ("antiid", antiid), ("blockones", blockones_f), ("triu", triu)):
        kb.dma("sp", "const", t[:], hc[nm][:, :], writes=[nm])
    kb.dma("sp", "const", sel6[:], hc["sel6"][:, :], writes=["sel6"])
    kb.op("dve", lambda e: e.tensor_copy(out=ident_b[:], in_=ident[:]), reads=["ident"], writes=["ident_b"])
    kb.op("dve", lambda e: e.tensor_copy(out=blockones[:], in_=blockones_f[:]), reads=["blockones"], writes=["blockones_b"])
    kb.op("dve", lambda e: e.tensor_copy(out=triu_b[:], in_=triu[:]), reads=["triu"], writes=["triu_b"])
    kb.op("dve", lambda e: e.memset(ones_b[:], 1.0), writes=["ones_b"])
    kb.op("dve", lambda e: e.memset(ones_f[:], 1.0), writes=["ones_f"])
    kb.op("dve", lambda e: e.memset(o128_b[:], 1.0 / 128), writes=["o128_b"])

    PS = [A("ps%d" % i, [128, 512], F32, psum=True) for i in range(5)]
    PSB = A("psb", [128, 1024], BF16, psum=True)
    ACCS = [A("acc%d" % i, [128, 512], F32, psum=True) for i in range(2)]
    PSB2 = ACCS[1][:].bitcast(BF16)
    PSB3 = PS[0][:].bitcast(BF16)
    PSB4 = PS[1][:].bitcast(BF16)
    ps_rr = [0]

    def next_ps():
        i = ps_rr[0] % 5
        ps_rr[0] += 1
        return PS[i], "ps%d" % i

    import os
    if 'setup' not in os.environ.get('K_DBG_SKIP', ''):
        m_setup = kb.mark()
        relb = A("relb", [35, 12], F32)
        kb.dma("sp", "const", relb[0:32, :], rel_bias[:, :], writes=["relb"])
        kb.dma("sp", "const", relb[32:35, :], hc["relb_tail"][:, :], writes=["relb2"])
        ohz = A("ohz", [35, NZ], F32)
        kb.dma("sp", "const", ohz[:], hc["ohz"][:, :], writes=["ohz"])
        zrow = A("zrow", [12, NZ], F32)
        for x0 in range(0, NZ, 512):
            n = min(512, NZ - x0)
            ps, pk = next_ps()
            kb.op("pe", lambda e: e.matmul(ps[0:12, 0:n], lhsT=relb[:, :], rhs=ohz[:, x0:x0 + n], start=True, stop=True),
                  reads=["relb", "relb2", "ohz"], writes=[pk])
            kb.op("act", lambda e: e.activation(out=zrow[:, x0:x0 + n], in_=ps[0:12, 0:n], func=AF.Exp), reads=[pk], writes=["zrow"])
        kb.dma("sp", "zt", zt[:, :], zrow[:], reads=["zrow"], writes=["zt"])
        hank = A("hank", [128, 17 * 128 + 0], F32)
        ebs = A("ebs", [128, 17 * 128], BF16)
        for h in range(12):
            src = bass.AP(zt.tensor, h * NZ, [[1, 128], [128, 17], [1, 128]])
            kb.dma("sp", "hank", hank[:].rearrange("p (o i) -> p o i", i=128), src, reads=["zt", "ebs_d"], writes=["hank"])
            for o0 in range(0, 17, 4):
                no = min(4, 17 - o0)
                ps, pk = next_ps()
                kb.op("pe", lambda e: e.matmul(ps[:, 0:no * 128], lhsT=antiid[:], rhs=hank[:, o0 * 128:(o0 + no) * 128], start=True, stop=True),
                      reads=["antiid", "hank"], writes=[pk])
                for oo in range(no):
                    j = 16 - (o0 + oo)
                    kb.op("dve", lambda e: e.tensor_copy(out=ebs[:, j * 128:(j + 1) * 128], in_=ps[:, oo * 128:(oo + 1) * 128]),
                          reads=[pk], writes=["ebs"])
            kb.dma("sp", "ebs_d", ebd[h, :, :], ebs[:], reads=["ebs"], writes=["ebd", "ebs_d"])

        kb.release(m_setup)
    hT = A("hT", [128, KC, 1024], BF16)
    yT = A("yT", [128, KC, 1024], BF16)
    WB = [A("wb%d" % i, [128, KC, 128], BF16) for i in range(4)]
    stat = A("stat", [128, 8], F32)
    gsT = A("gsT", [128, 2, KC], F32); shT = A("shT", [128, 2, KC], F32); ngT = A("ngT", [128, KC], F32)
    qkg = A("qkg", [128, 2], F32)
    sgug_b = A("sgug_b", [128, 512], F32)
    Rg = A("Rg", [128, 4, 128], BF16)
    sgub_row = A("sgub_row", [1, 512], F32)
    ifb_b = A("ifb_b", [128, 12], F32)
    ctail = A("ctail", [128, 12, 3], F32)
    cwT = A("cwT", [128, 12, 4], F32); cbT = A("cbT", [128, 12], F32); hngT = A("hngT", [128, 6], F32)
    Cst = A("Cst", [128, 6, 128], F32); nst = A("nst", [128, 6], F32)
    mall = A("mall", [6, 9], F32)
    VW = [A("VwA", [128, 24, 128], BF16), A("VwB", [128, 24, 128], BF16)]
    PH = {}
    kb.op("dve", lambda e: e.memset(VW[0][:, :, 64:128], 1.0), writes=["VwA_ones"])
    kb.op("dve", lambda e: e.memset(VW[1][:, :, 0:64], 1.0), writes=["VwB_ones"])

    wrr = [0]

    def load_w(l, col0, ncols=128):
        i = wrr[0] % 4
        wrr[0] += 1
        wb, key = WB[i], "wb%d" % i
        src = w_in[l, :, col0:col0 + ncols].rearrange("(kc p) c -> p kc c", p=128)
        kb.dma("pool", key, wb[:, :, 0:ncols], src, writes=[key])
        return wb, key

    def proj_fm(l, col0, ntok, consumer, ncols=128):
        wb, wkey = load_w(l, col0, ncols)
        for t0 in range(0, ntok, 512):
            n = min(512, ntok - t0)
            ps, pk = next_ps()
            for k in range(KC):
                kb.op("pe", lambda e: e.matmul(ps[0:ncols, 0:n], lhsT=wb[:, k, 0:ncols], rhs=hT[:, k, t0:t0 + n],
                                                start=(k == 0), stop=(k == KC - 1)),
                      reads=[wkey, "hT"], writes=[pk])
            consumer(ps[0:ncols, 0:n], t0, n, pk)

    def proj_tm(l, col0, ncols, ntok, consumer):
        if ncols <= 128:
            wb, wkey = load_w(l, col0, ncols)
            wbs = [(wb, wkey, 0, ncols)]
        else:
            wbs = []
            for c0 in range(0, ncols, 128):
                wb, wkey = load_w(l, col0 + c0, 128)
                wbs.append((wb, wkey, c0, 128))
        ts = min(128, ntok)
        for ti in range(ntok // ts):
            ps, pk = next_ps()
            for (wb, wkey, c0, cn) in wbs:
                for k in range(KC):
                    kb.op("pe", lambda e: e.matmul(ps[0:ts, c0:c0 + cn], lhsT=hT[:, k, ti * ts:(ti + 1) * ts], rhs=wb[:, k, 0:cn],
                                                    start=(k == 0), stop=(k == KC - 1)),
                          reads=[wkey, "hT"], writes=[pk])
            consumer(ps[0:ts, 0:ncols], ti, ts, pk)

    def rsqrt_act(out, in_, scale, rk, wk_):
        kb.op("act", lambda e: e.activation(out=out, in_=in_, func=AF.Ln, bias=epsb[0:out.shape[0], 0:1], scale=scale), reads=rk + ["epsb"], writes=wk_)
        kb.op("act", lambda e: e.activation(out=out, in_=out, func=AF.Exp, scale=-0.5), reads=wk_, writes=wk_)

    epsb = A("epsb", [128, 1], F32)
    kb.op("dve", lambda e: e.memset(epsb[:], EPS), writes=["epsb"])
    oneb = A("oneb", [128, 1], F32)
    kb.op("dve", lambda e: e.memset(oneb[:], 1.0), writes=["oneb"])

    def ada_mod(l):
        m_ph = kb.mark()
        csil = A("csil", [128, KC, 2], F32)
        csil_b = A("csil_b", [128, KC, 2], BF16)
        adaf = [A("adaf%d" % i, [128, KC, 512], BF16) for i in range(2)]
        modg = [A("modg%d" % i, [2, 512], F32) for i in range(2)]
        with nc.allow_non_contiguous_dma(reason="small per-partition vectors"):
            for r_ in range(2):
                kb.dma("sp", "csil", csil[:, :, r_], c_in[r_, :].rearrange("(kc p) -> p kc", p=128), writes=["csil"])
        kb.op("act", lambda e: e.activation(out=csil_b[:], in_=csil[:], func=AF.Silu), reads=["csil"], writes=["csil_b"])
        for g in range(12):
            mg, mk = modg[g % 2], "modg%d" % (g % 2)
            kb.dma("sp", mk + "b", mg[:], bass.AP(ada_b.tensor, l * 3 * D + g * 512, [[0, 2], [1, 512]]), writes=[mk])
            af, akey = adaf[g % 2], "adaf%d" % (g % 2)
            kb.dma("pool", akey, af[:], ada_w[l, :, g * 512:(g + 1) * 512].rearrange("(kc p) c -> p kc c", p=128), writes=[akey])
            ps, pk = next_ps()
            for k in range(KC):
                kb.op("pe", lambda e: e.matmul(ps[0:2, :], lhsT=csil_b[:, k, :], rhs=af[:, k, :], start=(k == 0), stop=(k == KC - 1)),
                      reads=["csil_b", akey], writes=[pk])
            kb.op("dve", lambda e: e.tensor_tensor(out=mg[:], in0=mg[:], in1=ps[0:2, :], op=ALU.add), reads=[pk, mk], writes=[mk])
            kb.dma("sp", mk + "o", modd[:, g * 512:(g + 1) * 512], mg[:], reads=[mk], writes=["modd_%d" % g])
        allg = ["modd_%d" % g for g in range(12)]
        with nc.allow_non_contiguous_dma(reason="small per-partition vectors"):
            for r_ in range(2):
                kb.dma("sp", "shT", shT[:, r_, :], modd[r_, 0:D].rearrange("(kc p) -> p kc", p=128), reads=allg, writes=["shT"])
                kb.dma("sp", "gsT", gsT[:, r_, :], modd[r_, D:2 * D].rearrange("(kc p) -> p kc", p=128), reads=allg, writes=["gsT"])
            kb.dma("sp", "ngT", ngT[:], norm_g[l, :].rearrange("(kc p) -> p kc", p=128), writes=["ngT"])
        for r in range(2):
            kb.op("dve", lambda e: e.scalar_tensor_tensor(out=gsT[:, r, :], in0=gsT[:, r, :], scalar=1.0, in1=ngT[:], op0=ALU.add, op1=ALU.mult),
                  reads=["gsT", "ngT"], writes=["gsT"])
        kb.op("dve", lambda e: e.memset(stat[:, 7:8], 0.0), reads=allg, writes=["modd"])
        kb.release(m_ph)

    def load_layer_consts(l):
        with nc.allow_non_contiguous_dma(reason="small per-partition vectors"):
            kb.dma("sp", "lc", qkg[0:64, 0:1], qn_g[l, :].rearrange("(p o) -> p o", o=1), writes=["qkg"])
            kb.dma("sp", "lc", qkg[64:128, 0:1], qn_g[l, :].rearrange("(p o) -> p o", o=1), writes=["qkg"])
            kb.dma("sp", "lc", qkg[0:64, 1:2], kn_g[l, :].rearrange("(p o) -> p o", o=1), writes=["qkg"])
            kb.dma("sp", "lc", qkg[64:128, 1:2], kn_g[l, :].rearrange("(p o) -> p o", o=1), writes=["qkg"])
            for j_ in range(4):
                kb.dma("sp", "lc", cwT[:, :, j_], conv_w[l, j_, :].rearrange("(c p) -> p c", p=128), writes=["cwT"])
            kb.dma("sp", "lc", cbT[:], conv_b[l, :].rearrange("(c p) -> p c", p=128), writes=["cbT"])
            kb.dma("sp", "lc", hngT[:], hn_g[l, :].rearrange("(c p) -> p c", p=128), writes=["hngT"])
        kb.op("dve", lambda e: e.tensor_scalar(out=qkg[:, 0:1], in0=qkg[:, 0:1], scalar1=0.125, scalar2=None, op0=ALU.mult), reads=["qkg"], writes=["qkg"])
        kb.dma("sp", "lc", sgug_b[:], bass.AP(sgu_g.tensor, l * 512, [[0, 128], [1, 512]]), writes=["sgug_b"])
        kb.dma("sp", "lc", ifb_b[:, 0:6], bass.AP(i_bias.tensor, l * 6, [[0, 128], [1, 6]]), writes=["ifb_b"])
        kb.dma("sp", "lc", ifb_b[:, 6:12], bass.AP(f_bias.tensor, l * 6, [[0, 128], [1, 6]]), writes=["ifb_b"])
        kb.dma("sp", "lc", sgub_row[:], sgu_b[l:l + 1, :, :].rearrange("o g t -> o (g t)"), writes=["sgub_row"])
        m_ph = kb.mark()
        wsf = A("wsf", [128, 128], F32)
        for g in range(4):
            kb.dma("sp", "lc", wsf[:], sgu_w[l, g, :, :], writes=["wsf"])
            ps, pk = next_ps()
            kb.op("pe", lambda e: e.matmul(ps[:, 0:128], lhsT=wsf[:], rhs=ident[:], start=True, stop=True), reads=["wsf", "ident"], writes=[pk])
            kb.op("dve", lambda e: e.tensor_tensor(out=Rg[:, g, :], in0=ps[:, 0:128], in1=triu[:], op=ALU.mult), reads=[pk, "triu"], writes=["Rg"])
        kb.release(m_ph)

    def unit(l, s, sample):
        r = 1 if sample else 0
        NT = T if sample else 1024
        TS = min(128, NT)
        NTL = NT // TS
        xsrc_all = (x_s if l == 0 else xs_res) if sample else (x_p if l == 0 else y_p)
        tok0 = 0 if sample else s * 1024
        xdst = (y_s if l == L - 1 else xs_res) if sample else y_p
        xkey = "xs_d" if sample else "xp_d%d" % s
        last = (sample or s == NSUP - 1)

        m_ph = kb.mark()
        XIN = [A("xin%d" % i, [128, 2048], F32) for i in range(2)]
        XNB = [A("xnb%d" % i, [128, 2048], BF16) for i in range(2)]
        stat8 = A("stat8", [128, 8], F32)
        import os
        XIN3 = XIN + [A("xin2", [128, 2048], F32)]
        junk = A("junk", [128, 2048], BF16)

        def stage_a(ti):
            xin, xk = XIN3[ti % 3], "xin%d" % (ti % 3)
            kb.dma("sp", xk, xin[0:TS, :], xsrc_all[tok0 + ti * TS: tok0 + (ti + 1) * TS, :], reads=[xkey], writes=[xk])
            kb.op("dve", lambda e: e.scalar_tensor_tensor(out=junk[0:TS, :], in0=xin[0:TS, :], scalar=1.0, in1=xin[0:TS, :], op0=ALU.mult, op1=ALU.mult,
                                                         accum_out=stat8[0:TS, ti:ti + 1]), reads=[xk], writes=["junk", "stat_%d" % ti])
            rsqrt_act(stat8[0:TS, ti:ti + 1], stat8[0:TS, ti:ti + 1], 1.0 / D, ["stat_%d" % ti], ["stat_%d" % ti])

        def stage_b(ti):
            xin, xk = XIN3[ti % 3], "xin%d" % (ti % 3)
            xnb, xnk = XNB[ti % 2], "xnb%d" % (ti % 2)
            kb.op("dve", lambda e: e.tensor_scalar(out=xnb[0:TS, :], in0=xin[0:TS, :], scalar1=stat8[0:TS, ti:ti + 1], scalar2=None, op0=ALU.mult),
                  reads=[xk, "stat_%d" % ti], writes=[xnk])
            for k0 in range(0, KC, 8):
                if ti % 2 == 0:
                    tgt, tkey = (PSB, "psb") if k0 == 0 else (PSB2, "acc1")
                else:
                    tgt, tkey = (PSB3, "ps0") if k0 == 0 else (PSB4, "ps1")
                for k in range(k0, k0 + 8):
                    kb.op("pe", lambda e: e.transpose(tgt[:, (k - k0) * 128:(k - k0) * 128 + TS], xnb[0:TS, k * 128:(k + 1) * 128], ident_b[0:TS, 0:TS]),
                          reads=[xnk, "ident_b"], writes=[tkey])
                for k in range(k0, k0 + 8):
                    if k0 == 0:
                        kb.op("act", lambda e: e.activation(out=hT[:, k, ti * TS:(ti + 1) * TS], in_=tgt[:, (k - k0) * 128:(k - k0) * 128 + TS],
                                                             func=AF.Identity, bias=shT[:, r, k:k + 1], scale=gsT[:, r, k:k + 1]),
                              reads=[tkey, "shT", "gsT"], writes=["hT"])
                    else:
                        kb.op("dve", lambda e: e.tensor_scalar(out=hT[:, k, ti * TS:(ti + 1) * TS], in0=tgt[:, (k - k0) * 128:(k - k0) * 128 + TS],
                                                                scalar1=gsT[:, r, k:k + 1], scalar2=shT[:, r, k:k + 1], op0=ALU.mult, op1=ALU.add),
                              reads=[tkey, "shT", "gsT"], writes=["hT"])

        stage_a(0)
        for ti in range(NTL):
            if ti + 1 < NTL:
                stage_a(ti + 1)
            stage_b(ti)

        kb.release(m_ph)
        if cfg.stage < 4:
            raise _Stop()
        m_ph = kb.mark()
        qT = A("qT", [128, 1024], BF16); zaT = A("zaT", [128, 1024], BF16)
        kTc = A("kTc", [128, 1024], BF16)
        kTw = [A("kTw0", [128, 3072], BF16)]
        EBH = [A("ebh%d" % i, [128, 17 * 128], BF16) for i in range(2)]
        Pt = [A("Pt%d" % i, [128, 512], BF16) for i in range(3)]
        sqb = A("sqb", [128, 1024], BF16)
        rst = A("rst", [128, 1024], F32)
        vstage = A("vstage", [128, 8, 128], F32); vbf = A("vbf", [128, 8, 128], BF16)
        kstage = A("kstage", [128, 8, 128], F32)
        rden = A("rden", [128, 2, 128], F32); atmp = A("atmp", [128, 2, 128], F32)
        ckb = A("ckb", [128, 16, 128], BF16)
        nwin = 16 if sample else min(16, s * 8)
        for c in range(6):
            def k_cons(ps, t0, n, pk):
                kb.op("act", lambda e: e.activation(out=sqb[:, 0:n], in_=ps, func=AF.Square), reads=[pk], writes=["sqb"])
                ps2, pk2 = next_ps()
                kb.op("pe", lambda e: e.matmul(ps2[:, 0:n], lhsT=blockones[:], rhs=sqb[:, 0:n], start=True, stop=True), reads=["sqb", "blockones_b"], writes=[pk2])
                rsqrt_act(rst[:, 0:n], ps2[:, 0:n], 1.0, [pk2], ["rst"])
                kb.op("dve", lambda e: e.scalar_tensor_tensor(out=kTc[:, t0:t0 + n], in0=ps, scalar=qkg[:, 1:2], in1=rst[:, 0:n], op0=ALU.mult, op1=ALU.mult),
                      reads=[pk, "rst", "qkg"], writes=["kTc"])
            proj_fm(l, OFF["k_a"] + c * 128, NT, k_cons)
            if int(os.environ.get("K_DBG_ASTOP", "99")) == 1:
                raise _Stop()
            kcol0 = (SEQ if sample else tok0)
            kb.dma("sp", "ktd_w", ktd[c * 128:(c + 1) * 128, kcol0:kcol0 + NT], kTc[:, 0:NT], reads=["kTc"], writes=["ktd%d" % c])
            if sample or tok0 >= SEQ - WK:
                for ti in range(NTL):
                    kb.op("pe", lambda e: e.transpose(PSB[0:TS, 0:128], kTc[:, ti * TS:(ti + 1) * TS], ident_b[:, :]), reads=["kTc", "ident_b"], writes=["psb"])
                    kb.op("dve", lambda e: e.tensor_copy(out=kstage[0:TS, ti, :], in_=PSB[0:TS, 0:128]), reads=["psb"], writes=["kstage"])
                if sample:
                    kb.dma("sp", "kout", s_k[l, :, c * 128:(c + 1) * 128], kstage[0:TS, 0, :], reads=["kstage"], writes=["o_sk"])
                else:
                    o0 = tok0 - (SEQ - WK)
                    kb.dma("sp", "kout", p_k[l, o0:o0 + 1024, c * 128:(c + 1) * 128].rearrange("(t p) f -> p t f", p=128), kstage[:], reads=["kstage"], writes=["o_pk"])
            if int(os.environ.get("K_DBG_ASTOP", "99")) == 2:
                raise _Stop()
            def v_cons(ps, ti, ts, pk):
                if os.environ.get("K_DBG_NOVCONS"):
                    return
                kb.op("act", lambda e: e.activation(out=vstage[0:ts, ti, :], in_=ps, func=AF.Identity), reads=[pk], writes=["vstage"])
                kb.op("dve", lambda e: e.tensor_copy(out=vbf[0:ts, ti, :], in_=vstage[0:ts, ti, :]), reads=["vstage"], writes=["vbf"])
            proj_tm(l, OFF["v_a"] + c * 128, 128, NT, v_cons)
            if int(os.environ.get("K_DBG_ASTOP", "99")) == 7:
                raise _Stop()
            if sample:
                kb.dma("sp", "vout", s_v[l, :, c * 128:(c + 1) * 128], vstage[0:TS, 0, :], reads=["vstage"], writes=["o_sv"])
                kb.dma("sp", "vsd_w", vsd[SEQ:SEQ + T, c * 128:(c + 1) * 128], vbf[0:TS, 0, :], reads=["vbf"], writes=["vsd%d" % c])
            else:
                if tok0 >= SEQ - WK:
                    o0 = tok0 - (SEQ - WK)
                    kb.dma("sp", "vout", p_v[l, o0:o0 + 1024, c * 128:(c + 1) * 128].rearrange("(t p) f -> p t f", p=128), vstage[:], reads=["vstage"], writes=["o_pv"])
                kb.dma("sp", "vsd_w", vsd[tok0:tok0 + 1024, c * 128:(c + 1) * 128].rearrange("(t p) f -> p t f", p=128), vbf[:], reads=["vbf"], writes=["vsd%d" % c])
            if int(os.environ.get("K_DBG_ASTOP", "99")) == 3:
                raise _Stop()
            def q_cons(ps, t0, n, pk):
                kb.op("act", lambda e: e.activation(out=sqb[:, 0:n], in_=ps, func=AF.Square), reads=[pk], writes=["sqb"])
                ps2, pk2 = next_ps()
                kb.op("pe", lambda e: e.matmul(ps2[:, 0:n], lhsT=blockones[:], rhs=sqb[:, 0:n], start=True, stop=True), reads=["sqb", "blockones_b"], writes=[pk2])
                rsqrt_act(rst[:, 0:n], ps2[:, 0:n], 1.0, [pk2], ["rst"])
                kb.op("dve", lambda e: e.scalar_tensor_tensor(out=qT[:, t0:t0 + n], in0=ps, scalar=qkg[:, 0:1], in1=rst[:, 0:n], op0=ALU.mult, op1=ALU.mult),
                      reads=[pk, "rst", "qkg"], writes=["qT"])
            proj_fm(l, OFF["q_a"] + c * 128, NT, q_cons)
            def z_cons(ps, t0, n, pk):
                kb.op("act", lambda e: e.activation(out=zaT[:, t0:t0 + n], in_=ps, func=AF.Silu), reads=[pk], writes=["zaT"])
            proj_fm(l, OFF["z_a"] + c * 128, NT, z_cons)

            if int(os.environ.get("K_DBG_ASTOP", "99")) == 4:
                raise _Stop()
            kw = kTw[0]
            if sample:
                kb.dma("pool", "ckb", ckb[:], ck[l, :, c * 128:(c + 1) * 128].rearrange("(t p) f -> p t f", p=128), writes=["ckb"])
                for t0 in range(0, 16, 8):
                    for tt in range(8):
                        kb.op("pe", lambda e: e.transpose(PSB[:, tt * 128:(tt + 1) * 128], ckb[:, t0 + tt, :], ident_b[:, :]), reads=["ckb", "ident_b"], writes=["psb"])
                    kb.op("dve", lambda e: e.tensor_copy(out=kw[:, t0 * 128:(t0 + 8) * 128], in_=PSB[:, 0:1024]), reads=["psb"], writes=["kTw"])
                kb.dma("sp", "ktw_l", kw[:, 2048:2048 + T], ktd[c * 128:(c + 1) * 128, SEQ:SEQ + T], reads=["ktd%d" % c], writes=["kTw"])
            else:
                h0 = tok0 - nwin * 128
                kb.dma("sp", "ktw_l", kw[:, 0:(nwin + 8) * 128], ktd[c * 128:(c + 1) * 128, h0:tok0 + 1024], reads=["ktd%d" % c], writes=["kTw"])
            if int(os.environ.get("K_DBG_ASTOP", "99")) == 5:
                raise _Stop()
            for hh in range(2):
                h = 2 * c + hh
                vw, vkey = VW[hh], "Vw%d" % hh
                nrows = slice(hh * 64, hh * 64 + 64)
                drows = slice((1 - hh) * 64, (1 - hh) * 64 + 64)
                vcols = slice(hh * 64, hh * 64 + 64)
                if sample:
                    kb.dma("pool", vkey, vw[:, 0:16, vcols], cv[l, :, h * 64:(h + 1) * 64].rearrange("(t p) f -> p t f", p=128), writes=[vkey])
                    kb.dma("sp", vkey + "n", vw[0:T, 16, vcols], vsd[SEQ:SEQ + T, h * 64:(h + 1) * 64], reads=["vsd%d" % c], writes=[vkey])
                else:
                    h0 = tok0 - nwin * 128
                    kb.dma("sp", vkey, vw[:, 0:nwin + 8, vcols], vsd[h0:tok0 + 1024, h * 64:(h + 1) * 64].rearrange("(t p) f -> p t f", p=128),
                           reads=["vsd%d" % c], writes=[vkey])
                eb, ekey = EBH[h % 2], "ebh%d" % (h % 2)
                kb.dma("sp", ekey, eb[:], ebd[h, :, :], reads=["ebd"], writes=[ekey])
                items = []
                for qt in range(NTL):
                    wq = nwin + qt
                    wlist = list(range(max(0, wq - 16), wq + 1))
                    groups = [wlist[g0:g0 + 4] for g0 in range(0, len(wlist), 4)]
                    for gi, grp in enumerate(groups):
                        items.append((qt, wq, grp, gi == 0, gi == len(groups) - 1))

                def emit_front(idx, item):
                    qt, wq, grp, isf, isl = item
                    ps, pk = next_ps()
                    for gi_, w in enumerate(grp):
                        nk = TS if (sample and w == wq) else 128
                        kb.op("pe", lambda e: e.matmul(ps[0:nk, gi_ * 128:gi_ * 128 + TS], lhsT=kw[nrows, w * 128:w * 128 + nk],
                                                        rhs=qT[nrows, qt * TS:(qt + 1) * TS], start=True, stop=True),
                              reads=["kTw", "qT"], writes=[pk])
                    pt, ptk = Pt[idx % 3], "Pt%d" % (idx % 3)
                    if TS == 128:
                        ng = len(grp)
                        j0 = grp[0] - wq + 16
                        kb.op("act", lambda e: e.activation(out=pt[:, 0:ng * 128], in_=ps[:, 0:ng * 128], func=AF.Exp), reads=[pk], writes=[ptk])
                        kb.op("dve", lambda e: e.tensor_tensor(out=pt[:, 0:ng * 128], in0=pt[:, 0:ng * 128], in1=eb[:, j0 * 128:(j0 + ng) * 128], op=ALU.mult),
                              reads=[ptk, ekey], writes=[ptk])
                    else:
                        for gi_, w in enumerate(grp):
                            nk = TS if w == wq else 128
                            j = w - wq + 16
                            kb.op("act", lambda e: e.activation(out=pt[0:nk, gi_ * 128:gi_ * 128 + TS], in_=ps[0:nk, gi_ * 128:gi_ * 128 + TS], func=AF.Exp),
                                  reads=[pk], writes=[ptk])
                            kb.op("dve", lambda e: e.tensor_tensor(out=pt[0:nk, gi_ * 128:gi_ * 128 + TS], in0=pt[0:nk, gi_ * 128:gi_ * 128 + TS],
                                                                    in1=eb[0:nk, j * 128:j * 128 + TS], op=ALU.mult), reads=[ptk, ekey], writes=[ptk])
                    return pt, ptk

                def emit_back(idx, item, pt, ptk):
                    qt, wq, grp, isf, isl = item
                    a_ = qt % 2
                    ACC = ACCS[a_]
                    acc = ACC[:, 0:TS]
                    acck = "acc%d" % a_
                    for gi_, w in enumerate(grp):
                        nk = TS if (sample and w == wq) else 128
                        kb.op("pe", lambda e: e.matmul(acc, lhsT=vw[0:nk, w, :], rhs=pt[0:nk, gi_ * 128:gi_ * 128 + TS],
                                                        start=(isf and gi_ == 0), stop=(w == wq)),
                              reads=[vkey, "VwA_ones", "VwB_ones", ptk], writes=[acck])
                    if isl:
                        rk, tk = "rden%d" % a_, "atmp%d" % a_
                        kb.op("dve", lambda e: e.reciprocal(out=rden[nrows, a_, 0:TS], in_=ACC[drows, 0:TS]), reads=[acck], writes=[rk])
                        kb.op("dve", lambda e: e.tensor_tensor(out=atmp[nrows, a_, 0:TS], in0=ACC[nrows, 0:TS], in1=rden[nrows, a_, 0:TS], op=ALU.mult),
                              reads=[acck, rk], writes=[tk])
                        kb.op("dve", lambda e: e.tensor_tensor(out=yT[nrows, c, qt * TS:(qt + 1) * TS], in0=atmp[nrows, a_, 0:TS], in1=zaT[nrows, qt * TS:(qt + 1) * TS], op=ALU.mult),
                              reads=[tk, "zaT"], writes=["yT"])

                prev = None
                for idx, item in enumerate(items):
                    pt, ptk = emit_front(idx, item)
                    if prev is not None:
                        emit_back(*prev)
                    prev = (idx, item, pt, ptk)
                emit_back(*prev)

        kb.release(m_ph)
        if cfg.stage < 5:
            raise _Stop()
        m_ph = kb.mark()
        vn = A("vn", [128, 8, 512], BF16); vnf = A("vnf", [128, 512], F32)
        ubuf = A("ubuf", [128, 1024], F32); Gb = A("Gb", [128, 1024], F32)
        def vb_cons(ps, ti, ts, pk):
            kb.op("act", lambda e: e.activation(out=vnf[0:ts, :], in_=ps, func=AF.Square, accum_out=stat[0:ts, 1:2]), reads=[pk], writes=["vnf", "stat1"])
            rsqrt_act(stat[0:ts, 1:2], stat[0:ts, 1:2], 1.0 / 512, ["stat1"], ["stat1"])
            kb.op("dve", lambda e: e.scalar_tensor_tensor(out=vnf[0:ts, :], in0=ps, scalar=stat[0:ts, 1:2], in1=sgug_b[0:ts, :], op0=ALU.mult, op1=ALU.mult),
                  reads=[pk, "stat1", "sgug_b"], writes=["vnf"])
            kb.op("act", lambda e: e.activation(out=vn[0:ts, ti, :], in_=vnf[0:ts, :], func=AF.Identity), reads=["vnf"], writes=["vn"])
            if sample:
                kb.dma("sp", "sgu_o", s_sgu[l, :, :], vnf[0:ts, :], reads=["vnf"], writes=["o_sgu"])
        proj_tm(l, OFF["v_b"], 512, NT, vb_cons)
        for g in range(4):
            def u_cons(ps, t0, n, pk):
                kb.op("act", lambda e: e.activation(out=ubuf[:, t0:t0 + n], in_=ps, func=AF.Identity), reads=[pk], writes=["ubuf"])
            proj_fm(l, OFF["u_b"] + g * 128, NT, u_cons)

            def zb_cons(ps, t0, n, pk):
                kb.op("act", lambda e: e.activation(out=Gb[:, t0:t0 + n], in_=ps, func=AF.Silu), reads=[pk], writes=["Gb"])
                kb.op("dve", lambda e: e.tensor_tensor(out=Gb[:, t0:t0 + n], in0=Gb[:, t0:t0 + n], in1=ubuf[:, t0:t0 + n], op=ALU.mult), reads=["Gb", "ubuf"], writes=["Gb"])
            proj_fm(l, OFF["z_b"] + g * 128, NT, zb_cons)
            for ti in range(NTL):
                ps, pk = next_ps()
                kb.op("pe", lambda e: e.matmul(ps[:, 0:TS], lhsT=vn[0:TS, ti, g * 128:(g + 1) * 128], rhs=Rg[0:TS, g, 0:TS], start=True, stop=False),
                      reads=["vn", "Rg"], writes=[pk])
                kb.op("pe", lambda e: e.matmul(ps[:, 0:TS], lhsT=ones_f[0:1, :], rhs=sgub_row[0:1, g * 128:g * 128 + TS], start=False, stop=True),
                      reads=["ones_f", "sgub_row"], writes=[pk])
                kb.op("dve", lambda e: e.tensor_tensor(out=yT[:, 6 + g, ti * TS:(ti + 1) * TS], in0=ps[:, 0:TS], in1=Gb[:, ti * TS:(ti + 1) * TS], op=ALU.mult),
                      reads=[pk, "Gb"], writes=["yT"])

        kb.release(m_ph)
        if cfg.stage < 6:
            raise _Stop()
        m_ph = kb.mark()
        sqb = A("sqb", [128, 1024], BF16)
        rst = A("rst", [128, 1024], F32)
        gi = A("gi", [128, 8, 12], F32); spb = A("spb", [128, 8, 6], F32)
        cs_rows = A("cs_rows", [6, 8, 128], F32); g_rows = A("g_rows", [6, 8, 128], F32)
        Gall = A("Gall", [6, 8], F32); Mall = A("Mall", [6, 8], F32)
        nMb = A("nMb", [6, 8], F32); nM = A("nM", [6, 8], F32); wcr = A("wcr", [6, 8], F32)
        r_rows = A("r_rows", [6, 1024], F32); ef_rows = A("ef_rows", [6, 1024], F32)
        xp = A("xp", [128, 1024 + 3], F32)
        cacc = A("cacc", [128, 1024], F32)
        qc = A("qc", [128, 1024], BF16); kc_ = A("kc", [128, 1024], F32); ktil = A("ktil", [128, 1024], BF16)
        vcb = A("vcb", [128, 8, 128], BF16); Gc = A("Gc", [128, 1024], F32); osig = A("osig", [128, 1024], F32)
        efb = A("efb", [128, 1024], F32); wcb = A("wcb", [128, 8], F32)
        Cbf_all = A("Cbf_all", [128, 8, 128], BF16); nrep_all = A("nrep_all", [128, 8, 128], BF16)
        aT_all = A("aT_all", [128, 8, 128], BF16); ktm_all = A("ktm_all", [128, 8, 128], BF16)
        U_all = A("U_all", [128, 8, 129], F32)
        hbuf = A("hbuf", [128, 1024], F32); dnm = A("dnm", [128, 128], F32)
        convout = A("convout", [3, 1536], F32)
        if sample:
            with nc.allow_non_contiguous_dma(reason="tiny state loads"):
                for j_ in range(3):
                    kb.dma("sp", "stl", ctail[:, :, j_], st_conv[l, j_, :].rearrange("(c p) -> p c", p=128), writes=["ctail"])
                kb.dma("sp", "stl", nst[:], st_n[l, :, :].rearrange("h k -> k h"), writes=["nst"])
                kb.dma("sp", "stl", mall[:, 0:1], st_m[l, :].rearrange("(h o) -> h o", o=1), writes=["mall"])
            kb.dma("sp", "stl", Cst[:], st_C[l, :, :, :].rearrange("h k v -> k h v"), writes=["Cst"])
        elif s == 0:
            kb.op("dve", lambda e: e.memset(ctail[:], 0.0), writes=["ctail"])
            kb.op("dve", lambda e: e.memset(Cst[:], 0.0), writes=["Cst"])
            kb.op("dve", lambda e: e.memset(nst[:], 0.0), writes=["nst"])
            kb.op("dve", lambda e: e.memset(mall[:, 0:1], 0.0), writes=["mall"])
        else:
            kb.op("dve", lambda e: e.tensor_copy(out=mall[:, 0:1], in_=mall[:, 8:9]), reads=["mall"], writes=["mall"])

        def if_cons(ps, ti, ts, pk):
            kb.op("dve", lambda e: e.tensor_tensor(out=gi[0:ts, ti, :], in0=ps, in1=ifb_b[0:ts, :], op=ALU.add), reads=[pk, "ifb_b"], writes=["gi"])
        proj_tm(l, OFF["i_c"], 12, NT, if_cons)
        kb.op("act", lambda e: e.activation(out=spb[0:TS, 0:NTL, :], in_=gi[0:TS, 0:NTL, 6:12], func=AF.Exp, scale=-1.0), reads=["gi"], writes=["spb"])
        kb.op("act", lambda e: e.activation(out=spb[0:TS, 0:NTL, :], in_=spb[0:TS, 0:NTL, :], func=AF.Ln, bias=oneb[0:TS, 0:1], scale=1.0), reads=["spb", "oneb"], writes=["spb"])
        for ti in range(NTL):
            ps, pk = next_ps()
            kb.op("pe", lambda e: e.matmul(ps[0:6, 0:TS], lhsT=spb[0:TS, ti, :], rhs=triu[0:TS, 0:TS], start=True, stop=True), reads=["spb", "triu"], writes=[pk])
            kb.op("pe", lambda e: e.matmul(ps[0:6, 128:128 + TS], lhsT=gi[0:TS, ti, 0:6], rhs=ident[0:TS, 0:TS], start=True, stop=True), reads=["gi", "ident"], writes=[pk])
            kb.op("act", lambda e: e.activation(out=cs_rows[:, ti, 0:TS], in_=ps[0:6, 0:TS], func=AF.Identity), reads=[pk], writes=["cs_rows"])
            kb.op("dve", lambda e: e.tensor_tensor(out=g_rows[:, ti, 0:TS], in0=ps[0:6, 128:128 + TS], in1=cs_rows[:, ti, 0:TS], op=ALU.add), reads=[pk, "cs_rows"], writes=["g_rows"])
        kb.op("dve", lambda e: e.tensor_reduce(out=Gall[:, 0:NTL], in_=g_rows[:, 0:NTL, 0:TS], axis=AX.X, op=ALU.max), reads=["g_rows"], writes=["Gall"])
        for ti in range(NTL):
            kb.op("dve", lambda e: e.tensor_tensor(out=Mall[:, ti:ti + 1], in0=mall[:, ti:ti + 1], in1=Gall[:, ti:ti + 1], op=ALU.max), reads=["mall", "Gall"], writes=["Mall"])
            kb.op("dve", lambda e: e.tensor_tensor(out=mall[:, ti + 1:ti + 2], in0=Mall[:, ti:ti + 1], in1=cs_rows[:, ti, TS - 1:TS], op=ALU.subtract),
                  reads=["Mall", "cs_rows"], writes=["mall"])
        if NTL < 8:
            kb.op("dve", lambda e: e.tensor_copy(out=mall[:, 8:9], in_=mall[:, NTL:NTL + 1]), reads=["mall"], writes=["mall"])
        kb.op("dve", lambda e: e.tensor_scalar(out=nM[:, 0:NTL], in0=Mall[:, 0:NTL], scalar1=-1.0, scalar2=None, op0=ALU.mult), reads=["Mall"], writes=["nM"])
        kb.op("dve", lambda e: e.tensor_scalar(out=nMb[:, 0:NTL], in0=Mall[:, 0:NTL], scalar1=-1.0, scalar2=-0.5 * math.log(128.0), op0=ALU.mult, op1=ALU.add), reads=["Mall"], writes=["nMb"])
        kb.op("dve", lambda e: e.tensor_tensor(out=wcr[:, 0:NTL], in0=mall[:, 0:NTL], in1=Mall[:, 0:NTL], op=ALU.subtract), reads=["mall", "Mall"], writes=["wcr"])
        kb.op("act", lambda e: e.activation(out=wcr[:, 0:NTL], in_=wcr[:, 0:NTL], func=AF.Exp), reads=["wcr"], writes=["wcr"])
        for ti in range(NTL):
            kb.op("act", lambda e: e.activation(out=r_rows[:, ti * TS:(ti + 1) * TS], in_=g_rows[:, ti, 0:TS], func=AF.Exp, bias=nMb[:, ti:ti + 1], scale=1.0),
                  reads=["g_rows", "nMb"], writes=["r_rows"])
            kb.op("act", lambda e: e.activation(out=ef_rows[:, ti * TS:(ti + 1) * TS], in_=cs_rows[:, ti, 0:TS], func=AF.Exp, bias=nM[:, ti:ti + 1], scale=1.0),
                  reads=["cs_rows", "nM"], writes=["ef_rows"])

        for h in range(6):
            def conv_chunk(cc, dst, dkey):
                def cons(ps, t0, n, pk):
                    kb.op("act", lambda e: e.activation(out=xp[:, 3 + t0:3 + t0 + n], in_=ps, func=AF.Identity), reads=[pk], writes=["xp"])
                kb.op("dve", lambda e: e.tensor_copy(out=xp[:, 0:3], in_=ctail[:, cc, :]), reads=["ctail"], writes=["xp"])
                proj_fm(l, OFF["qk_c"] + cc * 128, NT, cons)
                kb.op("dve", lambda e: e.tensor_copy(out=ctail[:, cc, :], in_=xp[:, NT:NT + 3]), reads=["xp"], writes=["ctail"])
                kb.op("dve", lambda e: e.tensor_scalar(out=cacc[:, 0:NT], in0=xp[:, 0:NT], scalar1=cwT[:, cc, 0:1], scalar2=cbT[:, cc:cc + 1], op0=ALU.mult, op1=ALU.add),
                      reads=["xp", "cwT", "cbT"], writes=["cacc"])
                for j in range(1, 4):
                    kb.op("dve", lambda e: e.scalar_tensor_tensor(out=cacc[:, 0:NT], in0=xp[:, j:j + NT], scalar=cwT[:, cc, j:j + 1], in1=cacc[:, 0:NT], op0=ALU.mult, op1=ALU.add),
                          reads=["xp", "cwT", "cacc"], writes=["cacc"])
                kb.op("act", lambda e: e.activation(out=dst[:, 0:NT], in_=cacc[:, 0:NT], func=AF.Silu), reads=["cacc"], writes=[dkey])
                if last:
                    ps, pk = next_ps()
                    kb.op("pe", lambda e: e.matmul(ps[0:3, 0:128], lhsT=xp[:, NT:NT + 3], rhs=ident[:], start=True, stop=True), reads=["xp", "ident"], writes=[pk])
                    kb.op("dve", lambda e: e.tensor_copy(out=convout[:, cc * 128:(cc + 1) * 128], in_=ps[0:3, 0:128]), reads=[pk], writes=["convout"])
            conv_chunk(h, qc, "qc")
            conv_chunk(6 + h, kc_, "kc")

            def vc_cons(ps, ti, ts, pk):
                kb.op("act", lambda e: e.activation(out=vcb[0:ts, ti, :], in_=ps, func=AF.Identity), reads=[pk], writes=["vcb"])
            proj_tm(l, OFF["v_c"] + h * 128, 128, NT, vc_cons)

            def o_cons(ps, t0, n, pk):
                kb.op("act", lambda e: e.activation(out=osig[:, t0:t0 + n], in_=ps, func=AF.Sigmoid), reads=[pk], writes=["osig"])
            proj_fm(l, OFF["o_c"] + h * 128, NT, o_cons)

            def zc_cons(ps, t0, n, pk):
                kb.op("act", lambda e: e.activation(out=Gc[:, t0:t0 + n], in_=ps, func=AF.Silu), reads=[pk], writes=["Gc"])
                kb.op("dve", lambda e: e.tensor_tensor(out=Gc[:, t0:t0 + n], in0=Gc[:, t0:t0 + n], in1=osig[:, t0:t0 + n], op=ALU.mult), reads=["Gc", "osig"], writes=["Gc"])
            proj_fm(l, OFF["z_c"] + h * 128, NT, zc_cons)

            for t0 in range(0, NT, 512):
                n = min(512, NT - t0)
                ps, pk = next_ps()
                kb.op("pe", lambda e: e.matmul(ps[:, 0:n], lhsT=sel6[:, h * 128:(h + 1) * 128], rhs=r_rows[:, t0:t0 + n], start=True, stop=True), reads=["sel6", "r_rows"], writes=[pk])
                kb.op("dve", lambda e: e.tensor_tensor(out=ktil[:, t0:t0 + n], in0=kc_[:, t0:t0 + n], in1=ps[:, 0:n], op=ALU.mult), reads=[pk, "kc"], writes=["ktil"])
                ps, pk = next_ps()
                kb.op("pe", lambda e: e.matmul(ps[:, 0:n], lhsT=sel6[:, h * 128:(h + 1) * 128], rhs=ef_rows[:, t0:t0 + n], start=True, stop=True), reads=["sel6", "ef_rows"], writes=[pk])
                kb.op("act", lambda e: e.activation(out=efb[:, t0:t0 + n], in_=ps[:, 0:n], func=AF.Identity), reads=[pk], writes=["efb"])
            ps, pk = next_ps()
            kb.op("pe", lambda e: e.matmul(ps[:, 0:NTL], lhsT=sel6[:, h * 128:(h + 1) * 128], rhs=wcr[:, 0:NTL], start=True, stop=True), reads=["sel6", "wcr"], writes=[pk])
            kb.op("act", lambda e: e.activation(out=wcb[:, 0:NTL], in_=ps[:, 0:NTL], func=AF.Identity), reads=[pk], writes=["wcb"])

            for t4 in range(0, NTL, 4):
                ps, pk = next_ps()
                nn = min(4, NTL - t4)
                for j in range(nn):
                    ti = t4 + j
                    tsl = slice(ti * TS, (ti + 1) * TS)
                    kb.op("pe", lambda e: e.matmul(ps[0:TS, j * 128:j * 128 + TS], lhsT=ktil[:, tsl], rhs=qc[:, tsl], start=True, stop=True), reads=["ktil", "qc"], writes=[pk])
                for j in range(nn):
                    ti = t4 + j
                    kb.op("dve", lambda e: e.tensor_tensor(out=aT_all[0:TS, ti, 0:TS], in0=ps[0:TS, j * 128:j * 128 + TS], in1=triu[0:TS, 0:TS], op=ALU.mult),
                          reads=[pk, "triu"], writes=["aTa%d" % ti])
            for ti in range(NTL):
                tsl = slice(ti * TS, (ti + 1) * TS)
                kb.op("pe", lambda e: e.transpose(PSB[0:TS, ti * 128:(ti + 1) * 128], ktil[:, tsl], ident_b[:, :]), reads=["ktil", "ident_b"], writes=["psb"])
            kb.op("act", lambda e: e.activation(out=ktm_all[0:TS, 0:NTL, :], in_=PSB[0:TS, 0:NTL * 128].rearrange("p (t f) -> p t f", f=128), func=AF.Identity),
                  reads=["psb"], writes=["ktma"])
            for ti in range(NTL):
                ps3, pk3 = next_ps()
                kb.op("pe", lambda e: e.matmul(ps3[:, 0:128], lhsT=ktm_all[0:TS, ti, :], rhs=vcb[0:TS, ti, :], start=True, stop=True), reads=["ktma", "vcb"], writes=[pk3])
                kb.op("pe", lambda e: e.matmul(ps3[:, 128:129], lhsT=ktm_all[0:TS, ti, :], rhs=ones_b[0:TS, 0:1], start=True, stop=True), reads=["ktma", "ones_b"], writes=[pk3])
                kb.op("act", lambda e: e.activation(out=U_all[:, ti, :], in_=ps3[:, 0:129], func=AF.Identity), reads=[pk3], writes=["Ua%d" % ti])
            for ti in range(NTL):
                kb.op("dve", lambda e: e.tensor_scalar(out=Cst[:, h, :], in0=Cst[:, h, :], scalar1=wcb[:, ti:ti + 1], scalar2=None, op0=ALU.mult), reads=["Cst", "wcb"], writes=["Cst"])
                kb.op("dve", lambda e: e.tensor_scalar(out=nst[:, h:h + 1], in0=nst[:, h:h + 1], scalar1=wcb[:, ti:ti + 1], scalar2=None, op0=ALU.mult), reads=["nst", "wcb"], writes=["nst"])
                kb.op("dve", lambda e: e.tensor_copy(out=Cbf_all[:, ti, :], in_=Cst[:, h, :]), reads=["Cst"], writes=["Cbfa%d" % ti])
                kb.op("dve", lambda e: e.tensor_scalar(out=nrep_all[:, ti, :], in0=ones_f[:], scalar1=nst[:, h:h + 1], scalar2=None, op0=ALU.mult), reads=["nst", "ones_f"], writes=["nrepa%d" % ti])
                kb.op("dve", lambda e: e.tensor_tensor(out=Cst[:, h, :], in0=Cst[:, h, :], in1=U_all[:, ti, 0:128], op=ALU.add), reads=["Ua%d" % ti, "Cst"], writes=["Cst"])
                kb.op("dve", lambda e: e.tensor_tensor(out=nst[:, h:h + 1], in0=nst[:, h:h + 1], in1=U_all[:, ti, 128:129], op=ALU.add), reads=["Ua%d" % ti, "nst"], writes=["nst"])
            for ti in range(NTL):
                tsl = slice(ti * TS, (ti + 1) * TS)
                ps2, pk2 = next_ps()
                kb.op("pe", lambda e: e.matmul(ps2[:, 0:TS], lhsT=vcb[0:TS, ti, :], rhs=aT_all[0:TS, ti, 0:TS], start=True, stop=False), reads=["vcb", "aTa%d" % ti], writes=[pk2])
                kb.op("pe", lambda e: e.matmul(ps2[:, 0:TS], lhsT=Cbf_all[:, ti, :], rhs=qc[:, tsl], start=False, stop=True), reads=["Cbfa%d" % ti, "qc"], writes=[pk2])
                kb.op("pe", lambda e: e.matmul(ps2[:, 128:128 + TS], lhsT=ones_b[0:TS, :], rhs=aT_all[0:TS, ti, 0:TS], start=True, stop=False), reads=["ones_b", "aTa%d" % ti], writes=[pk2])
                kb.op("pe", lambda e: e.matmul(ps2[:, 128:128 + TS], lhsT=nrep_all[:, ti, :], rhs=qc[:, tsl], start=False, stop=True), reads=["nrepa%d" % ti, "qc"], writes=[pk2])
                kb.op("act", lambda e: e.activation(out=dnm[:, 0:TS], in_=ps2[:, 128:128 + TS], func=AF.Abs), reads=[pk2], writes=["dnm"])
                kb.op("dve", lambda e: e.tensor_tensor(out=dnm[:, 0:TS], in0=dnm[:, 0:TS], in1=efb[:, tsl], op=ALU.max), reads=["dnm", "efb"], writes=["dnm"])
                kb.op("dve", lambda e: e.reciprocal(out=dnm[:, 0:TS], in_=dnm[:, 0:TS]), reads=["dnm"], writes=["dnm"])
                kb.op("dve", lambda e: e.tensor_tensor(out=hbuf[:, tsl], in0=ps2[:, 0:TS], in1=dnm[:, 0:TS], op=ALU.mult), reads=[pk2, "dnm"], writes=["hbuf"])
            kb.op("act", lambda e: e.activation(out=sqb[:, 0:NT], in_=hbuf[:, 0:NT], func=AF.Square), reads=["hbuf"], writes=["sqb"])
            for t0 in range(0, NT, 512):
                n = min(512, NT - t0)
                ps, pk = next_ps()
                kb.op("pe", lambda e: e.matmul(ps[:, 0:n], lhsT=o128_b[:], rhs=sqb[:, t0:t0 + n], start=True, stop=True), reads=["sqb", "o128_b"], writes=[pk])
                rsqrt_act(rst[:, t0:t0 + n], ps[:, 0:n], 1.0, [pk], ["rst"])
            kb.op("dve", lambda e: e.scalar_tensor_tensor(out=hbuf[:, 0:NT], in0=hbuf[:, 0:NT], scalar=hngT[:, h:h + 1], in1=rst[:, 0:NT], op0=ALU.mult, op1=ALU.mult),
                  reads=["hbuf", "hngT", "rst"], writes=["hbuf"])
            kb.op("dve", lambda e: e.tensor_tensor(out=yT[:, 10 + h, 0:NT], in0=hbuf[:, 0:NT], in1=Gc[:, 0:NT], op=ALU.mult), reads=["hbuf", "Gc"], writes=["yT"])

        if last:
            oc, on, om, ocv = (s_C, s_n, s_m, s_conv) if sample else (p_C, p_n, p_m, p_conv)
            kb.dma("sp", "st_o", oc[l, :, :, :].rearrange("h k v -> k h v"), Cst[:], reads=["Cst"], writes=["o_C%d" % r])
            with nc.allow_non_contiguous_dma(reason="tiny state stores"):
                kb.dma("sp", "st_o", on[l, :, :].rearrange("h k -> k h"), nst[:], reads=["nst"], writes=["o_n%d" % r])
                kb.dma("sp", "st_o", om[l, :].rearrange("(h o) -> h o", o=1), mall[:, 8:9], reads=["mall"], writes=["o_m%d" % r])
            kb.dma("sp", "st_o", ocv[l, :, :], convout[:], reads=["convout"], writes=["o_cv%d" % r])

        kb.release(m_ph)
        if cfg.stage < 7:
            raise _Stop()
        m_ph = kb.mark()
        WO = [A("wo%d" % i, [128, KC, 512], BF16) for i in range(2)]
        gate_b = A("gate_b", [128, D], F32)
        ostage = [A("ost%d" % i, [128, 512], F32) for i in range(2)]
        xres = [A("xres%d" % i, [128, 512], F32) for i in range(2)]
        kb.dma("sp", "gate_b", gate_b[:], bass.AP(modd.tensor, r * 3 * D + 2 * D, [[0, 128], [1, D]]), reads=["modd"], writes=["gate_b"])
        for cg in range(4):
            wo, wokey = WO[cg % 2], "wo%d" % (cg % 2)
            kb.dma("pool", wokey, wo[:], w_out[l, :, cg * 512:(cg + 1) * 512].rearrange("(kc p) c -> p kc c", p=128), writes=[wokey])
            for ti in range(NTL):
                ps, pk = next_ps()
                for k in range(KC):
                    kb.op("pe", lambda e: e.matmul(ps[0:TS, 0:512], lhsT=yT[:, k, ti * TS:(ti + 1) * TS], rhs=wo[:, k, :], start=(k == 0), stop=(k == KC - 1)),
                          reads=["yT", wokey], writes=[pk])
                i2 = (cg * NTL + ti) % 2
                xr, xrk = xres[i2], "xres%d" % i2
                os_, osk = ostage[i2], "ost%d" % i2
                kb.dma("sp", xrk, xr[0:TS, :], xsrc_all[tok0 + ti * TS:tok0 + (ti + 1) * TS, cg * 512:(cg + 1) * 512], reads=[xkey], writes=[xrk])
                kb.op("dve", lambda e: e.tensor_tensor(out=os_[0:TS, :], in0=ps[0:TS, 0:512], in1=gate_b[0:TS, cg * 512:(cg + 1) * 512], op=ALU.mult),
                      reads=[pk, "gate_b"], writes=[osk])
                kb.op("dve", lambda e: e.tensor_tensor(out=os_[0:TS, :], in0=os_[0:TS, :], in1=xr[0:TS, :], op=ALU.add), reads=[osk, xrk], writes=[osk])
                kb.dma("sp", osk, xdst[tok0 + ti * TS:tok0 + (ti + 1) * TS, cg * 512:(cg + 1) * 512], os_[0:TS, :], reads=[osk], writes=[xkey + "_w%d_%d" % (cg, ti)])
        wl = [xkey + "_w%d_%d" % (cg, ti) for cg in range(4) for ti in range(NTL)]
        kb.op("dve", lambda e: e.memset(stat[:, 7:8], 0.0), reads=wl, writes=[xkey])
        kb.release(m_ph)

    try:
        if cfg.stage < 2:
            raise _Stop()
        import os
        skip = os.environ.get("K_DBG_SKIP", "")
        for l in range(L):
            if "ada" not in skip:
                ada_mod(l)
            if "lc" not in skip:
                load_layer_consts(l)
            if cfg.stage < 3:
                raise _Stop()
            for s in range(NSUP):
                unit(l, s, False)
            unit(l, 0, True)
    except _Stop:
        pass
    kb.finish("sp")
    return nc


_CACHE = {}


def kernel(**inputs):
    cfg = Cfg()
    x_prompt = np.asarray(inputs["x_prompt"], np.float32)
    B, SEQ, _ = x_prompt.shape
    DB, T, _ = inputs["x_sample"].shape
    L = inputs["w_in"].shape[0]
    cfg = Cfg(depth=L, seq=SEQ, nsamp_tok=T, win=inputs["cache_k_win"].shape[2])
    key = (L, SEQ, T, Cfg.stage)
    if key not in _CACHE:
        _CACHE[key] = build_program(cfg)
    nc = _CACHE[key]
    hcst = host_consts()
    f = lambda a: np.ascontiguousarray(np.asarray(a, np.float32))
    shared = {k: f(inputs[k]) for k in ["rel_bias", "norm_g", "ada_w", "ada_b", "w_in", "qn_g", "kn_g", "sgu_g", "sgu_w", "sgu_b",
                                        "conv_w", "conv_b", "f_bias", "i_bias", "hn_g", "w_out"]}
    in_maps = []
    import os
    n = int(os.environ.get('K_NCORES', '8'))
    for c in range(n):
        b = c % B
        m = dict(shared)
        m["x_p"] = f(x_prompt[b])
        m["x_s"] = f(inputs["x_sample"][c])
        m["c_in"] = f(np.stack([inputs["c_prompt"][b], inputs["c_sample"][c]]))
        m["ck"] = f(np.asarray(inputs["cache_k_win"])[:, c].reshape(L, -1, 768))
        m["cv"] = f(np.asarray(inputs["cache_v_win"])[:, c].reshape(L, -1, 768))
        m["st_conv"] = f(np.asarray(inputs["state_conv"])[:, c])
        m["st_C"] = f(np.asarray(inputs["state_C"])[:, c])
        m["st_n"] = f(np.asarray(inputs["state_n"])[:, c])
        m["st_m"] = f(np.asarray(inputs["state_m"])[:, c])
        for k, v in hcst.items():
            m["c_" + k] = v
        in_maps.append(m)
    res = run_bass_kernel_spmd(nc, in_maps, core_ids=list(range(n)))
    R = res.results
    WK = cfg.wkeep
    st = lambda name, cores: np.stack([np.asarray(R[c][name], np.float32) for c in cores])
    pc = list(range(min(B, n)))
    sc = list(range(n))
    y_prompt = st("y_p", pc)
    y_sample = st("y_s", sc)
    p_k = st("p_k", pc).transpose(1, 0, 2, 3).reshape(L, len(pc), WK, 12, 64)
    p_v = st("p_v", pc).transpose(1, 0, 2, 3).reshape(L, len(pc), WK, 12, 64)
    p_conv = st("p_conv", pc).transpose(1, 0, 2, 3)
    p_C = st("p_C", pc).transpose(1, 0, 2, 3, 4)
    p_n = st("p_n", pc).transpose(1, 0, 2, 3)
    p_m = st("p_m", pc).transpose(1, 0, 2)
    s_k = st("s_k", sc).transpose(1, 0, 2, 3).reshape(L, n, T, 12, 64)
    s_v = st("s_v", sc).transpose(1, 0, 2, 3).reshape(L, n, T, 12, 64)
    s_sgu = st("s_sgu", sc).transpose(1, 0, 2, 3)
    s_conv = st("s_conv", sc).transpose(1, 0, 2, 3)
    s_C = st("s_C", sc).transpose(1, 0, 2, 3, 4)
    s_n = st("s_n", sc).transpose(1, 0, 2, 3)
    s_m = st("s_m", sc).transpose(1, 0, 2)
    outs = (y_prompt, y_sample, p_k, p_v, p_conv, p_C, p_n, p_m, s_k, s_v, s_sgu, s_conv, s_C, s_n, s_m)
    return tuple(np.ascontiguousarray(o, dtype=np.float32) for o in outs)
```

```python
import math
import numpy as np
import ml_dtypes
import concourse.bass as bass
import concourse.mybir as mybir
from concourse.bass_utils import run_bass_kernel_spmd

F32 = mybir.dt.float32
BF16 = mybir.dt.bfloat16
AF = mybir.ActivationFunctionType
ALU = mybir.AluOpType
AX = mybir.AxisListType

D = 2048
KC = 16
D_A, D_B, D_C = 768, 512, 768
H_A, H_C = 12, 6
D_IN = 8460
EPS = 1e-6
OFF = dict(q_a=0, k_a=768, v_a=1536, z_a=2304, u_b=3072, v_b=3584, z_b=4096, qk_c=4608,
           v_c=6144, o_c=6912, z_c=7680, i_c=8448, f_c=8454)
NZ = 17 * 128 + 256


class _Stop(Exception):
    pass


class Cfg:
    stage = 99

    def __init__(self, depth=4, seq=4096, nsamp_tok=8, win=2048):
        self.depth, self.seq, self.T, self.win = depth, seq, nsamp_tok, win
        self.nsup = seq // 1024
        self.wkeep = min(2048, seq)


class KB:
    def __init__(self, nc):
        self.nc = nc
        self.E = dict(pe=nc.tensor, act=nc.scalar, dve=nc.vector, pool=nc.gpsimd, sp=nc.sync)
        self.sems = {}
        self.cnt = {}
        self.waited = {}
        self.res = {}
        self.ctx = []
        self.semctx = []
        for e in self.E:
            self._sem("E_" + e)

    def _sem(self, key):
        if key not in self.sems:
            cm = self.nc.semaphore("s%d" % len(self.sems))
            self.sems[key] = cm.__enter__()
            self.semctx.append(cm)
            self.cnt[key] = 0
        return self.sems[key]

    def alloc(self, name, shape, dt, psum=False):
        self.nalloc = getattr(self, "nalloc", 0) + 1
        cm = (self.nc.psum_tensor if psum else self.nc.sbuf_tensor)("%s_%d" % (name, self.nalloc), list(shape), dt)
        t = cm.__enter__()
        self.ctx.append(cm)
        return t

    def _wait(self, eng, deps):
        for (skey, val) in deps:
            if skey == "E_" + eng and eng == "pe":
                continue
            if skey.startswith("D_"):
                val = max(val, self.cnt[skey])
            k = (eng, skey)
            if self.waited.get(k, 0) >= val:
                continue
            self.E[eng].wait_ge(self.sems[skey], val)
            self.waited[k] = val

    def _deps(self, reads, writes):
        deps = {}
        for k in reads:
            r = self.res.get(k)
            if r and r["w"]:
                s, v = r["w"]
                deps[s] = max(deps.get(s, 0), v)
        for k in writes:
            r = self.res.get(k)
            if r:
                if r["w"]:
                    s, v = r["w"]
                    deps[s] = max(deps.get(s, 0), v)
                for s, v in r["r"].items():
                    deps[s] = max(deps.get(s, 0), v)
        return list(deps.items())

    def _commit(self, tok, reads, writes):
        for k in writes:
            self.res[k] = {"w": tok, "r": {}}
        for k in reads:
            r = self.res.setdefault(k, {"w": None, "r": {}})
            r["r"][tok[0]] = max(r["r"].get(tok[0], 0), tok[1])

    def op(self, eng, fn, reads=(), writes=()):
        self._wait(eng, self._deps(reads, writes))
        ins = fn(self.E[eng])
        skey = "E_" + eng
        ins.then_inc(self.sems[skey], 1)
        self.cnt[skey] += 1
        self._commit((skey, self.cnt[skey]), reads, writes)

    def dma(self, q, stream, out, in_, reads=(), writes=(), **kw):
        skey = "D_" + stream
        self._sem(skey)
        deps = [d for d in self._deps(reads, writes) if d[0] != skey]
        self._wait(q, deps)
        ins = self.E[q].dma_start(out=out, in_=in_, **kw)
        ins.then_inc(self.sems[skey], 16)
        self.cnt[skey] += 16
        self._commit((skey, self.cnt[skey]), reads, writes)

    def mark(self):
        return len(self.ctx)

    def barrier(self):
        deps = [(s, c) for s, c in self.cnt.items() if c > 0]
        for e in self.E:
            self._wait(e, deps)

    def release(self, m):
        self.barrier()
        while len(self.ctx) > m:
            self.ctx.pop().__exit__(None, None, None)

    def finish(self, eng="sp"):
        deps = [(s, c) for s, c in self.cnt.items() if s.startswith("D_") and c > 0]
        deps += [(s, c) for s, c in self.cnt.items() if s.startswith("E_") and c > 0]
        self._wait(eng, deps)


def host_consts():
    c = {}
    c["ident"] = np.eye(128, dtype=np.float32)
    c["antiid"] = np.ascontiguousarray(np.eye(128, dtype=np.float32)[::-1])
    bo = np.zeros((128, 128), np.float32)
    bo[:64, :64] = 1.0 / 64
    bo[64:, 64:] = 1.0 / 64
    c["blockones"] = bo
    c["triu"] = np.triu(np.ones((128, 128), np.float32))
    sel = np.zeros((6, 6, 128), np.float32)
    for h in range(6):
        sel[h, h, :] = 1.0
    c["sel6"] = sel.reshape(6, 768)
    oh = np.zeros((35, NZ), np.float32)
    for x in range(NZ):
        d = x - 127
        mult = 0
        if 0 <= d <= 2048:
            mult = int(d <= 128) + int(d % 4 == 0 and d <= 512) + int(d % 16 == 0)
        if mult == 0:
            oh[32, x] = 1.0
            continue
        if d < 16:
            b = d
        else:
            b = min(16 + int(np.float32(np.log(np.float32(d) / np.float32(16)) / np.float32(math.log(2048 / 16)) * np.float32(16))), 31)
        oh[b, x] = 1.0
        if mult == 2:
            oh[33, x] = 1.0
        if mult == 3:
            oh[34, x] = 1.0
    c["ohz"] = oh
    tail = np.zeros((3, 12), np.float32)
    tail[0] = -30000.0
    tail[1] = math.log(2.0)
    tail[2] = math.log(3.0)
    c["relb_tail"] = tail
    return c


def _bucket_check():
    import jax.numpy as jnp
    return True


def build_program(cfg):
    nc = bass.Bass("TRN2", target_bir_lowering=False)
    kb = KB(nc)
    L, SEQ, T, NSUP = cfg.depth, cfg.seq, cfg.T, cfg.nsup
    WK = cfg.wkeep
    WIN = cfg.win

    def din(name, shape, dt=F32):
        return nc.dram_tensor(name, list(shape), dt, kind="ExternalInput").ap()

    def dout(name, shape):
        return nc.dram_tensor(name, list(shape), F32, kind="ExternalOutput").ap()

    def dscr(name, shape, dt):
        return nc.dram_tensor(name, list(shape), dt, kind="Internal").ap()

    x_p = din("x_p", [SEQ, D]); x_s = din("x_s", [T, D])
    c_in = din("c_in", [2, D])
    ck = din("ck", [L, WIN, 768]); cv = din("cv", [L, WIN, 768])
    st_conv = din("st_conv", [L, 3, 1536]); st_C = din("st_C", [L, 6, 128, 128])
    st_n = din("st_n", [L, 6, 128]); st_m = din("st_m", [L, 6])
    rel_bias = din("rel_bias", [32, 12]); norm_g = din("norm_g", [L, D])
    ada_w = din("ada_w", [L, D, 3 * D]); ada_b = din("ada_b", [L, 3 * D])
    w_in = din("w_in", [L, D, D_IN]); qn_g = din("qn_g", [L, 64]); kn_g = din("kn_g", [L, 64])
    sgu_g = din("sgu_g", [L, 512]); sgu_w = din("sgu_w", [L, 4, 128, 128]); sgu_b = din("sgu_b", [L, 4, 128])
    conv_w = din("conv_w", [L, 4, 1536]); conv_b = din("conv_b", [L, 1536])
    f_bias = din("f_bias", [L, 6]); i_bias = din("i_bias", [L, 6]); hn_g = din("hn_g", [L, 768])
    w_out = din("w_out", [L, D, D])
    hc = {k: din("c_" + k, list(v.shape)) for k, v in host_consts().items()}

    y_p = dout("y_p", [SEQ, D]); y_s = dout("y_s", [T, D])
    p_k = dout("p_k", [L, WK, 768]); p_v = dout("p_v", [L, WK, 768])
    p_conv = dout("p_conv", [L, 3, 1536]); p_C = dout("p_C", [L, 6, 128, 128])
    p_n = dout("p_n", [L, 6, 128]); p_m = dout("p_m", [L, 6])
    s_k = dout("s_k", [L, T, 768]); s_v = dout("s_v", [L, T, 768]); s_sgu = dout("s_sgu", [L, T, 512])
    s_conv = dout("s_conv", [L, 3, 1536]); s_C = dout("s_C", [L, 6, 128, 128])
    s_n = dout("s_n", [L, 6, 128]); s_m = dout("s_m", [L, 6])

    modd = dscr("modd", [2, 3 * D], F32)
    zt = dscr("zt", [12, NZ], F32)
    ebd = dscr("ebd", [12, 128, 17 * 128], BF16)
    ktd = dscr("ktd", [768, SEQ + 8], BF16)
    vsd = dscr("vsd", [SEQ + 8, 768], BF16)
    xs_res = dscr("xs_res", [T, D], F32)

    A = kb.alloc
    ident = A("ident", [128, 128], F32); antiid = A("antiid", [128, 128], F32)
    blockones_f = A("blockones_f", [128, 128], F32); triu = A("triu", [128, 128], F32)
    ident_b = A("ident_b", [128, 128], BF16); blockones = A("blockones", [128, 128], BF16)
    ones_b = A("ones_b", [128, 128], BF16); o128_b = A("o128_b", [128, 128], BF16)
    triu_b = A("triu_b", [128, 128], BF16)
    sel6 = A("sel6", [6, 768], F32)
    ones_f = A("ones_f", [128, 128], F32)
    for nm, t in (("ident", ident), ("antiid", antiid), ("blockones", blockones_f), ("triu", triu)):
        kb.dma("sp", "const", t[:], hc[nm][:, :], writes=[nm])
    kb.dma("sp", "const", sel6[:], hc["sel6"][:, :], writes=["sel6"])
    kb.op("dve", lambda e: e.tensor_copy(out=ident_b[:], in_=ident[:]), reads=["ident"], writes=["ident_b"])
    kb.op("dve", lambda e: e.tensor_copy(out=blockones[:], in_=blockones_f[:]), reads=["blockones"], writes=["blockones_b"])
    kb.op("dve", lambda e: e.tensor_copy(out=triu_b[:], in_=triu[:]), reads=["triu"], writes=["triu_b"])
    kb.op("dve", lambda e: e.memset(ones_b[:], 1.0), writes=["ones_b"])
    kb.op("dve", lambda e: e.memset(ones_f[:], 1.0), writes=["ones_f"])
    kb.op("dve", lambda e: e.memset(o128_b[:], 1.0 / 128), writes=["o128_b"])

    PS = [A("ps%d" % i, [128, 512], F32, psum=True) for i in range(5)]
    PSB = A("psb", [128, 1024], BF16, psum=True)
    ACCS = [A("acc%d" % i, [128, 512], F32, psum=True) for i in range(2)]
    PSB2 = ACCS[1][:].bitcast(BF16)
    ps_rr = [0]

    def next_ps():
        i = ps_rr[0] % 5
        ps_rr[0] += 1
        return PS[i], "ps%d" % i

    import os
    if 'setup' not in os.environ.get('K_DBG_SKIP', ''):
        m_setup = kb.mark()
        relb = A("relb", [35, 12], F32)
        kb.dma("sp", "const", relb[0:32, :], rel_bias[:, :], writes=["relb"])
        kb.dma("sp", "const", relb[32:35, :], hc["relb_tail"][:, :], writes=["relb2"])
        ohz = A("ohz", [35, NZ], F32)
        kb.dma("sp", "const", ohz[:], hc["ohz"][:, :], writes=["ohz"])
        zrow = A("zrow", [12, NZ], F32)
        for x0 in range(0, NZ, 512):
            n = min(512, NZ - x0)
            ps, pk = next_ps()
            kb.op("pe", lambda e: e.matmul(ps[0:12, 0:n], lhsT=relb[:, :], rhs=ohz[:, x0:x0 + n], start=True, stop=True),
                  reads=["relb", "relb2", "ohz"], writes=[pk])
            kb.op("act", lambda e: e.activation(out=zrow[:, x0:x0 + n], in_=ps[0:12, 0:n], func=AF.Exp), reads=[pk], writes=["zrow"])
        kb.dma("sp", "zt", zt[:, :], zrow[:], reads=["zrow"], writes=["zt"])
        hank = A("hank", [128, 17 * 128 + 0], F32)
        ebs = A("ebs", [128, 17 * 128], BF16)
        for h in range(12):
            src = bass.AP(zt.tensor, h * NZ, [[1, 128], [128, 17], [1, 128]])
            kb.dma("sp", "hank", hank[:].rearrange("p (o i) -> p o i", i=128), src, reads=["zt", "ebs_d"], writes=["hank"])
            for o0 in range(0, 17, 4):
                no = min(4, 17 - o0)
                ps, pk = next_ps()
                kb.op("pe", lambda e: e.matmul(ps[:, 0:no * 128], lhsT=antiid[:], rhs=hank[:, o0 * 128:(o0 + no) * 128], start=True, stop=True),
                      reads=["antiid", "hank"], writes=[pk])
                for oo in range(no):
                    j = 16 - (o0 + oo)
                    kb.op("dve", lambda e: e.tensor_copy(out=ebs[:, j * 128:(j + 1) * 128], in_=ps[:, oo * 128:(oo + 1) * 128]),
                          reads=[pk], writes=["ebs"])
            kb.dma("sp", "ebs_d", ebd[h, :, :], ebs[:], reads=["ebs"], writes=["ebd", "ebs_d"])

        kb.release(m_setup)
    hT = A("hT", [128, KC, 1024], BF16)
    yT = A("yT", [128, KC, 1024], BF16)
    WB = [A("wb%d" % i, [128, KC, 128], BF16) for i in range(4)]
    stat = A("stat", [128, 8], F32)
    gsT = A("gsT", [128, 2, KC], F32); shT = A("shT", [128, 2, KC], F32); ngT = A("ngT", [128, KC], F32)
    qkg = A("qkg", [128, 2], F32)
    sgug_b = A("sgug_b", [128, 512], F32)
    Rg = A("Rg", [128, 4, 128], BF16)
    sgub_row = A("sgub_row", [1, 512], F32)
    ifb_b = A("ifb_b", [128, 12], F32)
    ctail = A("ctail", [128, 12, 3], F32)
    cwT = A("cwT", [128, 12, 4], F32); cbT = A("cbT", [128, 12], F32); hngT = A("hngT", [128, 6], F32)
    Cst = A("Cst", [128, 6, 128], F32); nst = A("nst", [128, 6], F32)
    mall = A("mall", [6, 9], F32)
    VW = [A("VwA", [128, 24, 128], BF16), A("VwB", [128, 24, 128], BF16)]
    PH = {}
    kb.op("dve", lambda e: e.memset(VW[0][:, :, 64:128], 1.0), writes=["VwA_ones"])
    kb.op("dve", lambda e: e.memset(VW[1][:, :, 0:64], 1.0), writes=["VwB_ones"])

    wrr = [0]

    def load_w(l, col0, ncols=128):
        i = wrr[0] % 4
        wrr[0] += 1
        wb, key = WB[i], "wb%d" % i
        src = w_in[l, :, col0:col0 + ncols].rearrange("(kc p) c -> p kc c", p=128)
        kb.dma("pool", key, wb[:, :, 0:ncols], src, writes=[key])
        return wb, key

    def proj_fm(l, col0, ntok, consumer, ncols=128):
        wb, wkey = load_w(l, col0, ncols)
        for t0 in range(0, ntok, 512):
            n = min(512, ntok - t0)
            ps, pk = next_ps()
            for k in range(KC):
                kb.op("pe", lambda e: e.matmul(ps[0:ncols, 0:n], lhsT=wb[:, k, 0:ncols], rhs=hT[:, k, t0:t0 + n],
                                                start=(k == 0), stop=(k == KC - 1)),
                      reads=[wkey, "hT"], writes=[pk])
            consumer(ps[0:ncols, 0:n], t0, n, pk)

    def proj_tm(l, col0, ncols, ntok, consumer):
        if ncols <= 128:
            wb, wkey = load_w(l, col0, ncols)
            wbs = [(wb, wkey, 0, ncols)]
        else:
            wbs = []
            for c0 in range(0, ncols, 128):
                wb, wkey = load_w(l, col0 + c0, 128)
                wbs.append((wb, wkey, c0, 128))
        ts = min(128, ntok)
        for ti in range(ntok // ts):
            ps, pk = next_ps()
            for (wb, wkey, c0, cn) in wbs:
                for k in range(KC):
                    kb.op("pe", lambda e: e.matmul(ps[0:ts, c0:c0 + cn], lhsT=hT[:, k, ti * ts:(ti + 1) * ts], rhs=wb[:, k, 0:cn],
                                                    start=(k == 0), stop=(k == KC - 1)),
                          reads=[wkey, "hT"], writes=[pk])
            consumer(ps[0:ts, 0:ncols], ti, ts, pk)

    def rsqrt_act(out, in_, scale, rk, wk_):
        kb.op("act", lambda e: e.activation(out=out, in_=in_, func=AF.Ln, bias=epsb[0:out.shape[0], 0:1], scale=scale), reads=rk + ["epsb"], writes=wk_)
        kb.op("act", lambda e: e.activation(out=out, in_=out, func=AF.Exp, scale=-0.5), reads=wk_, writes=wk_)

    epsb = A("epsb", [128, 1], F32)
    kb.op("dve", lambda e: e.memset(epsb[:], EPS), writes=["epsb"])
    oneb = A("oneb", [128, 1], F32)
    kb.op("dve", lambda e: e.memset(oneb[:], 1.0), writes=["oneb"])

    def ada_mod(l):
        m_ph = kb.mark()
        csil = A("csil", [128, KC, 2], F32)
        csil_b = A("csil_b", [128, KC, 2], BF16)
        adaf = [A("adaf%d" % i, [128, KC, 512], BF16) for i in range(2)]
        modg = [A("modg%d" % i, [2, 512], F32) for i in range(2)]
        with nc.allow_non_contiguous_dma(reason="small per-partition vectors"):
            for r_ in range(2):
                kb.dma("sp", "csil", csil[:, :, r_], c_in[r_, :].rearrange("(kc p) -> p kc", p=128), writes=["csil"])
        kb.op("act", lambda e: e.activation(out=csil_b[:], in_=csil[:], func=AF.Silu), reads=["csil"], writes=["csil_b"])
        for g in range(12):
            mg, mk = modg[g % 2], "modg%d" % (g % 2)
            kb.dma("sp", mk + "b", mg[:], bass.AP(ada_b.tensor, l * 3 * D + g * 512, [[0, 2], [1, 512]]), writes=[mk])
            af, akey = adaf[g % 2], "adaf%d" % (g % 2)
            kb.dma("pool", akey, af[:], ada_w[l, :, g * 512:(g + 1) * 512].rearrange("(kc p) c -> p kc c", p=128), writes=[akey])
            ps, pk = next_ps()
            for k in range(KC):
                kb.op("pe", lambda e: e.matmul(ps[0:2, :], lhsT=csil_b[:, k, :], rhs=af[:, k, :], start=(k == 0), stop=(k == KC - 1)),
                      reads=["csil_b", akey], writes=[pk])
            kb.op("dve", lambda e: e.tensor_tensor(out=mg[:], in0=mg[:], in1=ps[0:2, :], op=ALU.add), reads=[pk, mk], writes=[mk])
            kb.dma("sp", mk + "o", modd[:, g * 512:(g + 1) * 512], mg[:], reads=[mk], writes=["modd_%d" % g])
        allg = ["modd_%d" % g for g in range(12)]
        with nc.allow_non_contiguous_dma(reason="small per-partition vectors"):
            for r_ in range(2):
                kb.dma("sp", "shT", shT[:, r_, :], modd[r_, 0:D].rearrange("(kc p) -> p kc", p=128), reads=allg, writes=["shT"])
                kb.dma("sp", "gsT", gsT[:, r_, :], modd[r_, D:2 * D].rearrange("(kc p) -> p kc", p=128), reads=allg, writes=["gsT"])
            kb.dma("sp", "ngT", ngT[:], norm_g[l, :].rearrange("(kc p) -> p kc", p=128), writes=["ngT"])
        for r in range(2):
            kb.op("dve", lambda e: e.scalar_tensor_tensor(out=gsT[:, r, :], in0=gsT[:, r, :], scalar=1.0, in1=ngT[:], op0=ALU.add, op1=ALU.mult),
                  reads=["gsT", "ngT"], writes=["gsT"])
        kb.op("dve", lambda e: e.memset(stat[:, 7:8], 0.0), reads=allg, writes=["modd"])
        kb.release(m_ph)

    def load_layer_consts(l):
        with nc.allow_non_contiguous_dma(reason="small per-partition vectors"):
            kb.dma("sp", "lc", qkg[0:64, 0:1], qn_g[l, :].rearrange("(p o) -> p o", o=1), writes=["qkg"])
            kb.dma("sp", "lc", qkg[64:128, 0:1], qn_g[l, :].rearrange("(p o) -> p o", o=1), writes=["qkg"])
            kb.dma("sp", "lc", qkg[0:64, 1:2], kn_g[l, :].rearrange("(p o) -> p o", o=1), writes=["qkg"])
            kb.dma("sp", "lc", qkg[64:128, 1:2], kn_g[l, :].rearrange("(p o) -> p o", o=1), writes=["qkg"])
            for j_ in range(4):
                kb.dma("sp", "lc", cwT[:, :, j_], conv_w[l, j_, :].rearrange("(c p) -> p c", p=128), writes=["cwT"])
            kb.dma("sp", "lc", cbT[:], conv_b[l, :].rearrange("(c p) -> p c", p=128), writes=["cbT"])
            kb.dma("sp", "lc", hngT[:], hn_g[l, :].rearrange("(c p) -> p c", p=128), writes=["hngT"])
        kb.op("dve", lambda e: e.tensor_scalar(out=qkg[:, 0:1], in0=qkg[:, 0:1], scalar1=0.125, scalar2=None, op0=ALU.mult), reads=["qkg"], writes=["qkg"])
        kb.dma("sp", "lc", sgug_b[:], bass.AP(sgu_g.tensor, l * 512, [[0, 128], [1, 512]]), writes=["sgug_b"])
        kb.dma("sp", "lc", ifb_b[:, 0:6], bass.AP(i_bias.tensor, l * 6, [[0, 128], [1, 6]]), writes=["ifb_b"])
        kb.dma("sp", "lc", ifb_b[:, 6:12], bass.AP(f_bias.tensor, l * 6, [[0, 128], [1, 6]]), writes=["ifb_b"])
        kb.dma("sp", "lc", sgub_row[:], sgu_b[l:l + 1, :, :].rearrange("o g t -> o (g t)"), writes=["sgub_row"])
        m_ph = kb.mark()
        wsf = A("wsf", [128, 128], F32)
        for g in range(4):
            kb.dma("sp", "lc", wsf[:], sgu_w[l, g, :, :], writes=["wsf"])
            ps, pk = next_ps()
            kb.op("pe", lambda e: e.matmul(ps[:, 0:128], lhsT=wsf[:], rhs=ident[:], start=True, stop=True), reads=["wsf", "ident"], writes=[pk])
            kb.op("dve", lambda e: e.tensor_tensor(out=Rg[:, g, :], in0=ps[:, 0:128], in1=triu[:], op=ALU.mult), reads=[pk, "triu"], writes=["Rg"])
        kb.release(m_ph)

    def unit(l, s, sample):
        r = 1 if sample else 0
        NT = T if sample else 1024
        TS = min(128, NT)
        NTL = NT // TS
        xsrc_all = (x_s if l == 0 else xs_res) if sample else (x_p if l == 0 else y_p)
        tok0 = 0 if sample else s * 1024
        xdst = (y_s if l == L - 1 else xs_res) if sample else y_p
        xkey = "xs_d" if sample else "xp_d%d" % s
        last = (sample or s == NSUP - 1)

        m_ph = kb.mark()
        XIN = [A("xin%d" % i, [128, 2048], F32) for i in range(2)]
        XNB = [A("xnb%d" % i, [128, 2048], BF16) for i in range(2)]
        import os
        for ti in range(min(NTL, int(os.environ.get("K_DBG_NTL", "99")))):
            xin, xk = XIN[ti % 2], "xin%d" % (ti % 2)
            xnb, xnk = XNB[ti % 2], "xnb%d" % (ti % 2)
            kb.dma("sp", xk, xin[0:TS, :], xsrc_all[tok0 + ti * TS: tok0 + (ti + 1) * TS, :], reads=[xkey], writes=[xk])
            kb.op("dve", lambda e: e.scalar_tensor_tensor(out=xnb[0:TS, :], in0=xin[0:TS, :], scalar=1.0, in1=xin[0:TS, :], op0=ALU.mult, op1=ALU.mult,
                                                         accum_out=stat[0:TS, 0:1]), reads=[xk], writes=[xnk, "stat"])
            rsqrt_act(stat[0:TS, 0:1], stat[0:TS, 0:1], 1.0 / D, ["stat"], ["stat"])
            kb.op("dve", lambda e: e.tensor_scalar(out=xnb[0:TS, :], in0=xin[0:TS, :], scalar1=stat[0:TS, 0:1], scalar2=None, op0=ALU.mult),
                  reads=[xk, "stat"], writes=[xnk])
            for k0 in range(0, KC if os.environ.get("K_DBG_NOTR") is None else 0, 8):
                tgt, tkey = (PSB, "psb") if k0 == 0 else (PSB2, "acc1")
                for k in range(k0, k0 + 8):
                    kb.op("pe", lambda e: e.transpose(tgt[:, (k - k0) * 128:(k - k0) * 128 + TS], xnb[0:TS, k * 128:(k + 1) * 128], ident_b[0:TS, 0:TS]),
                          reads=[xnk, "ident_b"], writes=[tkey])
                for k in range(k0, k0 + 8):
                    if k0 == 0:
                        kb.op("act", lambda e: e.activation(out=hT[:, k, ti * TS:(ti + 1) * TS], in_=tgt[:, (k - k0) * 128:(k - k0) * 128 + TS],
                                                             func=AF.Identity, bias=shT[:, r, k:k + 1], scale=gsT[:, r, k:k + 1]),
                              reads=[tkey, "shT", "gsT"], writes=["hT"])
                    else:
                        kb.op("dve", lambda e: e.tensor_scalar(out=hT[:, k, ti * TS:(ti + 1) * TS], in0=tgt[:, (k - k0) * 128:(k - k0) * 128 + TS],
                                                                scalar1=gsT[:, r, k:k + 1], scalar2=shT[:, r, k:k + 1], op0=ALU.mult, op1=ALU.add),
                              reads=[tkey, "shT", "gsT"], writes=["hT"])

        kb.release(m_ph)
        if cfg.stage < 4:
            raise _Stop()
        m_ph = kb.mark()
        qT = A("qT", [128, 1024], BF16); zaT = A("zaT", [128, 1024], BF16)
        kTc = A("kTc", [128, 1024], BF16)
        kTw = [A("kTw0", [128, 3072], BF16)]
        EBH = [A("ebh%d" % i, [128, 17 * 128], BF16) for i in range(2)]
        Pt = [A("Pt%d" % i, [128, 512], BF16) for i in range(5)]
        sqb = A("sqb", [128, 1024], BF16)
        rst = A("rst", [128, 1024], F32)
        vstage = A("vstage", [128, 8, 128], F32); vbf = A("vbf", [128, 8, 128], BF16)
        kstage = A("kstage", [128, 8, 128], F32)
        rden = A("rden", [128, 2, 128], F32); atmp = A("atmp", [128, 2, 128], F32)
        ckb = A("ckb", [128, 16, 128], BF16)
        nwin = 16 if sample else min(16, s * 8)
        for c in range(6):
            def k_cons(ps, t0, n, pk):
                kb.op("act", lambda e: e.activation(out=sqb[:, 0:n], in_=ps, func=AF.Square), reads=[pk], writes=["sqb"])
                ps2, pk2 = next_ps()
                kb.op("pe", lambda e: e.matmul(ps2[:, 0:n], lhsT=blockones[:], rhs=sqb[:, 0:n], start=True, stop=True), reads=["sqb", "blockones_b"], writes=[pk2])
                rsqrt_act(rst[:, 0:n], ps2[:, 0:n], 1.0, [pk2], ["rst"])
                kb.op("dve", lambda e: e.scalar_tensor_tensor(out=kTc[:, t0:t0 + n], in0=ps, scalar=qkg[:, 1:2], in1=rst[:, 0:n], op0=ALU.mult, op1=ALU.mult),
                      reads=[pk, "rst", "qkg"], writes=["kTc"])
            proj_fm(l, OFF["k_a"] + c * 128, NT, k_cons)
            if int(os.environ.get("K_DBG_ASTOP", "99")) == 1:
                raise _Stop()
            kcol0 = (SEQ if sample else tok0)
            kb.dma("sp", "ktd_w", ktd[c * 128:(c + 1) * 128, kcol0:kcol0 + NT], kTc[:, 0:NT], reads=["kTc"], writes=["ktd%d" % c])
            if sample or tok0 >= SEQ - WK:
                for ti in range(NTL):
                    kb.op("pe", lambda e: e.transpose(PSB[0:TS, 0:128], kTc[:, ti * TS:(ti + 1) * TS], ident_b[:, :]), reads=["kTc", "ident_b"], writes=["psb"])
                    kb.op("dve", lambda e: e.tensor_copy(out=kstage[0:TS, ti, :], in_=PSB[0:TS, 0:128]), reads=["psb"], writes=["kstage"])
                if sample:
                    kb.dma("sp", "kout", s_k[l, :, c * 128:(c + 1) * 128], kstage[0:TS, 0, :], reads=["kstage"], writes=["o_sk"])
                else:
                    o0 = tok0 - (SEQ - WK)
                    kb.dma("sp", "kout", p_k[l, o0:o0 + 1024, c * 128:(c + 1) * 128].rearrange("(t p) f -> p t f", p=128), kstage[:], reads=["kstage"], writes=["o_pk"])
            if int(os.environ.get("K_DBG_ASTOP", "99")) == 2:
                raise _Stop()
            def v_cons(ps, ti, ts, pk):
                if os.environ.get("K_DBG_NOVCONS"):
                    return
                kb.op("act", lambda e: e.activation(out=vstage[0:ts, ti, :], in_=ps, func=AF.Identity), reads=[pk], writes=["vstage"])
                kb.op("dve", lambda e: e.tensor_copy(out=vbf[0:ts, ti, :], in_=vstage[0:ts, ti, :]), reads=["vstage"], writes=["vbf"])
            proj_tm(l, OFF["v_a"] + c * 128, 128, NT, v_cons)
            if int(os.environ.get("K_DBG_ASTOP", "99")) == 7:
                raise _Stop()
            if sample:
                kb.dma("sp", "vout", s_v[l, :, c * 128:(c + 1) * 128], vstage[0:TS, 0, :], reads=["vstage"], writes=["o_sv"])
                kb.dma("sp", "vsd_w", vsd[SEQ:SEQ + T, c * 128:(c + 1) * 128], vbf[0:TS, 0, :], reads=["vbf"], writes=["vsd%d" % c])
            else:
                if tok0 >= SEQ - WK:
                    o0 = tok0 - (SEQ - WK)
                    kb.dma("sp", "vout", p_v[l, o0:o0 + 1024, c * 128:(c + 1) * 128].rearrange("(t p) f -> p t f", p=128), vstage[:], reads=["vstage"], writes=["o_pv"])
                kb.dma("sp", "vsd_w", vsd[tok0:tok0 + 1024, c * 128:(c + 1) * 128].rearrange("(t p) f -> p t f", p=128), vbf[:], reads=["vbf"], writes=["vsd%d" % c])
            if int(os.environ.get("K_DBG_ASTOP", "99")) == 3:
                raise _Stop()
            def q_cons(ps, t0, n, pk):
                kb.op("act", lambda e: e.activation(out=sqb[:, 0:n], in_=ps, func=AF.Square), reads=[pk], writes=["sqb"])
                ps2, pk2 = next_ps()
                kb.op("pe", lambda e: e.matmul(ps2[:, 0:n], lhsT=blockones[:], rhs=sqb[:, 0:n], start=True, stop=True), reads=["sqb", "blockones_b"], writes=[pk2])
                rsqrt_act(rst[:, 0:n], ps2[:, 0:n], 1.0, [pk2], ["rst"])
                kb.op("dve", lambda e: e.scalar_tensor_tensor(out=qT[:, t0:t0 + n], in0=ps, scalar=qkg[:, 0:1], in1=rst[:, 0:n], op0=ALU.mult, op1=ALU.mult),
                      reads=[pk, "rst", "qkg"], writes=["qT"])
            proj_fm(l, OFF["q_a"] + c * 128, NT, q_cons)
            def z_cons(ps, t0, n, pk):
                kb.op("act", lambda e: e.activation(out=zaT[:, t0:t0 + n], in_=ps, func=AF.Silu), reads=[pk], writes=["zaT"])
            proj_fm(l, OFF["z_a"] + c * 128, NT, z_cons)

            if int(os.environ.get("K_DBG_ASTOP", "99")) == 4:
                raise _Stop()
            kw = kTw[0]
            if sample:
                kb.dma("pool", "ckb", ckb[:], ck[l, :, c * 128:(c + 1) * 128].rearrange("(t p) f -> p t f", p=128), writes=["ckb"])
                for t0 in range(0, 16, 8):
                    for tt in range(8):
                        kb.op("pe", lambda e: e.transpose(PSB[:, tt * 128:(tt + 1) * 128], ckb[:, t0 + tt, :], ident_b[:, :]), reads=["ckb", "ident_b"], writes=["psb"])
                    kb.op("dve", lambda e: e.tensor_copy(out=kw[:, t0 * 128:(t0 + 8) * 128], in_=PSB[:, 0:1024]), reads=["psb"], writes=["kTw"])
                kb.dma("sp", "ktw_l", kw[:, 2048:2048 + T], ktd[c * 128:(c + 1) * 128, SEQ:SEQ + T], reads=["ktd%d" % c], writes=["kTw"])
            else:
                h0 = tok0 - nwin * 128
                kb.dma("sp", "ktw_l", kw[:, 0:(nwin + 8) * 128], ktd[c * 128:(c + 1) * 128, h0:tok0 + 1024], reads=["ktd%d" % c], writes=["kTw"])
            if int(os.environ.get("K_DBG_ASTOP", "99")) == 5:
                raise _Stop()
            for hh in range(2):
                h = 2 * c + hh
                vw, vkey = VW[hh], "Vw%d" % hh
                nrows = slice(hh * 64, hh * 64 + 64)
                drows = slice((1 - hh) * 64, (1 - hh) * 64 + 64)
                vcols = slice(hh * 64, hh * 64 + 64)
                if sample:
                    kb.dma("pool", vkey, vw[:, 0:16, vcols], cv[l, :, h * 64:(h + 1) * 64].rearrange("(t p) f -> p t f", p=128), writes=[vkey])
                    kb.dma("sp", vkey + "n", vw[0:T, 16, vcols], vsd[SEQ:SEQ + T, h * 64:(h + 1) * 64], reads=["vsd%d" % c], writes=[vkey])
                else:
                    h0 = tok0 - nwin * 128
                    kb.dma("sp", vkey, vw[:, 0:nwin + 8, vcols], vsd[h0:tok0 + 1024, h * 64:(h + 1) * 64].rearrange("(t p) f -> p t f", p=128),
                           reads=["vsd%d" % c], writes=[vkey])
                eb, ekey = EBH[h % 2], "ebh%d" % (h % 2)
                kb.dma("sp", ekey, eb[:], ebd[h, :, :], reads=["ebd"], writes=[ekey])
                items = []
                for qt in range(NTL):
                    wq = nwin + qt
                    wlist = list(range(max(0, wq - 16), wq + 1))
                    groups = [wlist[g0:g0 + 4] for g0 in range(0, len(wlist), 4)]
                    for gi, grp in enumerate(groups):
                        items.append((qt, wq, grp, gi == 0, gi == len(groups) - 1))

                def emit_front(idx, item):
                    qt, wq, grp, isf, isl = item
                    ps, pk = next_ps()
                    for gi_, w in enumerate(grp):
                        nk = TS if (sample and w == wq) else 128
                        kb.op("pe", lambda e: e.matmul(ps[0:nk, gi_ * 128:gi_ * 128 + TS], lhsT=kw[nrows, w * 128:w * 128 + nk],
                                                        rhs=qT[nrows, qt * TS:(qt + 1) * TS], start=True, stop=True),
                              reads=["kTw", "qT"], writes=[pk])
                    pt, ptk = Pt[idx % 5], "Pt%d" % (idx % 5)
                    if TS == 128:
                        ng = len(grp)
                        j0 = grp[0] - wq + 16
                        kb.op("act", lambda e: e.activation(out=pt[:, 0:ng * 128], in_=ps[:, 0:ng * 128], func=AF.Exp), reads=[pk], writes=[ptk])
                        kb.op("dve", lambda e: e.tensor_tensor(out=pt[:, 0:ng * 128], in0=pt[:, 0:ng * 128], in1=eb[:, j0 * 128:(j0 + ng) * 128], op=ALU.mult),
                              reads=[ptk, ekey], writes=[ptk])
                    else:
                        for gi_, w in enumerate(grp):
                            nk = TS if w == wq else 128
                            j = w - wq + 16
                            kb.op("act", lambda e: e.activation(out=pt[0:nk, gi_ * 128:gi_ * 128 + TS], in_=ps[0:nk, gi_ * 128:gi_ * 128 + TS], func=AF.Exp),
                                  reads=[pk], writes=[ptk])
                            kb.op("dve", lambda e: e.tensor_tensor(out=pt[0:nk, gi_ * 128:gi_ * 128 + TS], in0=pt[0:nk, gi_ * 128:gi_ * 128 + TS],
                                                                    in1=eb[0:nk, j * 128:j * 128 + TS], op=ALU.mult), reads=[ptk, ekey], writes=[ptk])
                    return pt, ptk

                def emit_back(idx, item, pt, ptk):
                    qt, wq, grp, isf, isl = item
                    a_ = qt % 2
                    ACC = ACCS[a_]
                    acc = ACC[:, 0:TS]
                    acck = "acc%d" % a_
                    for gi_, w in enumerate(grp):
                        nk = TS if (sample and w == wq) else 128
                        kb.op("pe", lambda e: e.matmul(acc, lhsT=vw[0:nk, w, :], rhs=pt[0:nk, gi_ * 128:gi_ * 128 + TS],
                                                        start=(isf and gi_ == 0), stop=(w == wq)),
                              reads=[vkey, "VwA_ones", "VwB_ones", ptk], writes=[acck])
                    if isl:
                        rk, tk = "rden%d" % a_, "atmp%d" % a_
                        kb.op("dve", lambda e: e.reciprocal(out=rden[nrows, a_, 0:TS], in_=ACC[drows, 0:TS]), reads=[acck], writes=[rk])
                        kb.op("dve", lambda e: e.tensor_tensor(out=atmp[nrows, a_, 0:TS], in0=ACC[nrows, 0:TS], in1=rden[nrows, a_, 0:TS], op=ALU.mult),
                              reads=[acck, rk], writes=[tk])
                        kb.op("dve", lambda e: e.tensor_tensor(out=yT[nrows, c, qt * TS:(qt + 1) * TS], in0=atmp[nrows, a_, 0:TS], in1=zaT[nrows, qt * TS:(qt + 1) * TS], op=ALU.mult),
                              reads=[tk, "zaT"], writes=["yT"])

                pending = []
                for idx, item in enumerate(items):
                    pt, ptk = emit_front(idx, item)
                    pending.append((idx, item, pt, ptk))
                    if len(pending) > 3:
                        emit_back(*pending.pop(0))
                while pending:
                    emit_back(*pending.pop(0))

        kb.release(m_ph)
        if cfg.stage < 5:
            raise _Stop()
        m_ph = kb.mark()
        vn = A("vn", [128, 8, 512], BF16); vnf = A("vnf", [128, 512], F32)
        ubuf = A("ubuf", [128, 1024], F32); Gb = A("Gb", [128, 1024], F32)
        def vb_cons(ps, ti, ts, pk):
            kb.op("act", lambda e: e.activation(out=vnf[0:ts, :], in_=ps, func=AF.Square, accum_out=stat[0:ts, 1:2]), reads=[pk], writes=["vnf", "stat1"])
            rsqrt_act(stat[0:ts, 1:2], stat[0:ts, 1:2], 1.0 / 512, ["stat1"], ["stat1"])
            kb.op("dve", lambda e: e.scalar_tensor_tensor(out=vnf[0:ts, :], in0=ps, scalar=stat[0:ts, 1:2], in1=sgug_b[0:ts, :], op0=ALU.mult, op1=ALU.mult),
                  reads=[pk, "stat1", "sgug_b"], writes=["vnf"])
            kb.op("act", lambda e: e.activation(out=vn[0:ts, ti, :], in_=vnf[0:ts, :], func=AF.Identity), reads=["vnf"], writes=["vn"])
            if sample:
                kb.dma("sp", "sgu_o", s_sgu[l, :, :], vnf[0:ts, :], reads=["vnf"], writes=["o_sgu"])
        proj_tm(l, OFF["v_b"], 512, NT, vb_cons)
        for g in range(4):
            def u_cons(ps, t0, n, pk):
                kb.op("act", lambda e: e.activation(out=ubuf[:, t0:t0 + n], in_=ps, func=AF.Identity), reads=[pk], writes=["ubuf"])
            proj_fm(l, OFF["u_b"] + g * 128, NT, u_cons)

            def zb_cons(ps, t0, n, pk):
                kb.op("act", lambda e: e.activation(out=Gb[:, t0:t0 + n], in_=ps, func=AF.Silu), reads=[pk], writes=["Gb"])
                kb.op("dve", lambda e: e.tensor_tensor(out=Gb[:, t0:t0 + n], in0=Gb[:, t0:t0 + n], in1=ubuf[:, t0:t0 + n], op=ALU.mult), reads=["Gb", "ubuf"], writes=["Gb"])
            proj_fm(l, OFF["z_b"] + g * 128, NT, zb_cons)
            for ti in range(NTL):
                ps, pk = next_ps()
                kb.op("pe", lambda e: e.matmul(ps[:, 0:TS], lhsT=vn[0:TS, ti, g * 128:(g + 1) * 128], rhs=Rg[0:TS, g, 0:TS], start=True, stop=False),
                      reads=["vn", "Rg"], writes=[pk])
                kb.op("pe", lambda e: e.matmul(ps[:, 0:TS], lhsT=ones_f[0:1, :], rhs=sgub_row[0:1, g * 128:g * 128 + TS], start=False, stop=True),
                      reads=["ones_f", "sgub_row"], writes=[pk])
                kb.op("dve", lambda e: e.tensor_tensor(out=yT[:, 6 + g, ti * TS:(ti + 1) * TS], in0=ps[:, 0:TS], in1=Gb[:, ti * TS:(ti + 1) * TS], op=ALU.mult),
                      reads=[pk, "Gb"], writes=["yT"])

        kb.release(m_ph)
        if cfg.stage < 6:
            raise _Stop()
        m_ph = kb.mark()
        sqb = A("sqb", [128, 1024], BF16)
        rst = A("rst", [128, 1024], F32)
        gi = A("gi", [128, 8, 12], F32); spb = A("spb", [128, 8, 6], F32)
        cs_rows = A("cs_rows", [6, 8, 128], F32); g_rows = A("g_rows", [6, 8, 128], F32)
        Gall = A("Gall", [6, 8], F32); Mall = A("Mall", [6, 8], F32)
        nMb = A("nMb", [6, 8], F32); nM = A("nM", [6, 8], F32); wcr = A("wcr", [6, 8], F32)
        r_rows = A("r_rows", [6, 1024], F32); ef_rows = A("ef_rows", [6, 1024], F32)
        xp = A("xp", [128, 1024 + 3], F32)
        cacc = A("cacc", [128, 1024], F32)
        qc = A("qc", [128, 1024], BF16); kc_ = A("kc", [128, 1024], F32); ktil = A("ktil", [128, 1024], BF16)
        vcb = A("vcb", [128, 8, 128], BF16); Gc = A("Gc", [128, 1024], F32); osig = A("osig", [128, 1024], F32)
        efb = A("efb", [128, 1024], F32); wcb = A("wcb", [128, 8], F32)
        Cbf_all = A("Cbf_all", [128, 8, 128], BF16); nrep_all = A("nrep_all", [128, 8, 128], BF16)
        aT_all = A("aT_all", [128, 8, 128], BF16); ktm_all = A("ktm_all", [128, 8, 128], BF16)
        U_all = A("U_all", [128, 8, 129], F32)
        hbuf = A("hbuf", [128, 1024], F32); dnm = A("dnm", [128, 128], F32)
        convout = A("convout", [3, 1536], F32)
        if sample:
            with nc.allow_non_contiguous_dma(reason="tiny state loads"):
                for j_ in range(3):
                    kb.dma("sp", "stl", ctail[:, :, j_], st_conv[l, j_, :].rearrange("(c p) -> p c", p=128), writes=["ctail"])
                kb.dma("sp", "stl", nst[:], st_n[l, :, :].rearrange("h k -> k h"), writes=["nst"])
                kb.dma("sp", "stl", mall[:, 0:1], st_m[l, :].rearrange("(h o) -> h o", o=1), writes=["mall"])
            kb.dma("sp", "stl", Cst[:], st_C[l, :, :, :].rearrange("h k v -> k h v"), writes=["Cst"])
        elif s == 0:
            kb.op("dve", lambda e: e.memset(ctail[:], 0.0), writes=["ctail"])
            kb.op("dve", lambda e: e.memset(Cst[:], 0.0), writes=["Cst"])
            kb.op("dve", lambda e: e.memset(nst[:], 0.0), writes=["nst"])
            kb.op("dve", lambda e: e.memset(mall[:, 0:1], 0.0), writes=["mall"])
        else:
            kb.op("dve", lambda e: e.tensor_copy(out=mall[:, 0:1], in_=mall[:, 8:9]), reads=["mall"], writes=["mall"])

        def if_cons(ps, ti, ts, pk):
            kb.op("dve", lambda e: e.tensor_tensor(out=gi[0:ts, ti, :], in0=ps, in1=ifb_b[0:ts, :], op=ALU.add), reads=[pk, "ifb_b"], writes=["gi"])
        proj_tm(l, OFF["i_c"], 12, NT, if_cons)
        kb.op("act", lambda e: e.activation(out=spb[0:TS, 0:NTL, :], in_=gi[0:TS, 0:NTL, 6:12], func=AF.Exp, scale=-1.0), reads=["gi"], writes=["spb"])
        kb.op("act", lambda e: e.activation(out=spb[0:TS, 0:NTL, :], in_=spb[0:TS, 0:NTL, :], func=AF.Ln, bias=oneb[0:TS, 0:1], scale=1.0), reads=["spb", "oneb"], writes=["spb"])
        for ti in range(NTL):
            ps, pk = next_ps()
            kb.op("pe", lambda e: e.matmul(ps[0:6, 0:TS], lhsT=spb[0:TS, ti, :], rhs=triu[0:TS, 0:TS], start=True, stop=True), reads=["spb", "triu"], writes=[pk])
            kb.op("pe", lambda e: e.matmul(ps[0:6, 128:128 + TS], lhsT=gi[0:TS, ti, 0:6], rhs=ident[0:TS, 0:TS], start=True, stop=True), reads=["gi", "ident"], writes=[pk])
            kb.op("act", lambda e: e.activation(out=cs_rows[:, ti, 0:TS], in_=ps[0:6, 0:TS], func=AF.Identity), reads=[pk], writes=["cs_rows"])
            kb.op("dve", lambda e: e.tensor_tensor(out=g_rows[:, ti, 0:TS], in0=ps[0:6, 128:128 + TS], in1=cs_rows[:, ti, 0:TS], op=ALU.add), reads=[pk, "cs_rows"], writes=["g_rows"])
        kb.op("dve", lambda e: e.tensor_reduce(out=Gall[:, 0:NTL], in_=g_rows[:, 0:NTL, 0:TS], axis=AX.X, op=ALU.max), reads=["g_rows"], writes=["Gall"])
        for ti in range(NTL):
            kb.op("dve", lambda e: e.tensor_tensor(out=Mall[:, ti:ti + 1], in0=mall[:, ti:ti + 1], in1=Gall[:, ti:ti + 1], op=ALU.max), reads=["mall", "Gall"], writes=["Mall"])
            kb.op("dve", lambda e: e.tensor_tensor(out=mall[:, ti + 1:ti + 2], in0=Mall[:, ti:ti + 1], in1=cs_rows[:, ti, TS - 1:TS], op=ALU.subtract),
                  reads=["Mall", "cs_rows"], writes=["mall"])
        if NTL < 8:
            kb.op("dve", lambda e: e.tensor_copy(out=mall[:, 8:9], in_=mall[:, NTL:NTL + 1]), reads=["mall"], writes=["mall"])
        kb.op("dve", lambda e: e.tensor_scalar(out=nM[:, 0:NTL], in0=Mall[:, 0:NTL], scalar1=-1.0, scalar2=None, op0=ALU.mult), reads=["Mall"], writes=["nM"])
        kb.op("dve", lambda e: e.tensor_scalar(out=nMb[:, 0:NTL], in0=Mall[:, 0:NTL], scalar1=-1.0, scalar2=-0.5 * math.log(128.0), op0=ALU.mult, op1=ALU.add), reads=["Mall"], writes=["nMb"])
        kb.op("dve", lambda e: e.tensor_tensor(out=wcr[:, 0:NTL], in0=mall[:, 0:NTL], in1=Mall[:, 0:NTL], op=ALU.subtract), reads=["mall", "Mall"], writes=["wcr"])
        kb.op("act", lambda e: e.activation(out=wcr[:, 0:NTL], in_=wcr[:, 0:NTL], func=AF.Exp), reads=["wcr"], writes=["wcr"])
        for ti in range(NTL):
            kb.op("act", lambda e: e.activation(out=r_rows[:, ti * TS:(ti + 1) * TS], in_=g_rows[:, ti, 0:TS], func=AF.Exp, bias=nMb[:, ti:ti + 1], scale=1.0),
                  reads=["g_rows", "nMb"], writes=["r_rows"])
            kb.op("act", lambda e: e.activation(out=ef_rows[:, ti * TS:(ti + 1) * TS], in_=cs_rows[:, ti, 0:TS], func=AF.Exp, bias=nM[:, ti:ti + 1], scale=1.0),
                  reads=["cs_rows", "nM"], writes=["ef_rows"])

        for h in range(6):
            def conv_chunk(cc, dst, dkey):
                def cons(ps, t0, n, pk):
                    kb.op("act", lambda e: e.activation(out=xp[:, 3 + t0:3 + t0 + n], in_=ps, func=AF.Identity), reads=[pk], writes=["xp"])
                kb.op("dve", lambda e: e.tensor_copy(out=xp[:, 0:3], in_=ctail[:, cc, :]), reads=["ctail"], writes=["xp"])
                proj_fm(l, OFF["qk_c"] + cc * 128, NT, cons)
                kb.op("dve", lambda e: e.tensor_copy(out=ctail[:, cc, :], in_=xp[:, NT:NT + 3]), reads=["xp"], writes=["ctail"])
                kb.op("dve", lambda e: e.tensor_scalar(out=cacc[:, 0:NT], in0=xp[:, 0:NT], scalar1=cwT[:, cc, 0:1], scalar2=cbT[:, cc:cc + 1], op0=ALU.mult, op1=ALU.add),
                      reads=["xp", "cwT", "cbT"], writes=["cacc"])
                for j in range(1, 4):
                    kb.op("dve", lambda e: e.scalar_tensor_tensor(out=cacc[:, 0:NT], in0=xp[:, j:j + NT], scalar=cwT[:, cc, j:j + 1], in1=cacc[:, 0:NT], op0=ALU.mult, op1=ALU.add),
                          reads=["xp", "cwT", "cacc"], writes=["cacc"])
                kb.op("act", lambda e: e.activation(out=dst[:, 0:NT], in_=cacc[:, 0:NT], func=AF.Silu), reads=["cacc"], writes=[dkey])
                if last:
                    ps, pk = next_ps()
                    kb.op("pe", lambda e: e.matmul(ps[0:3, 0:128], lhsT=xp[:, NT:NT + 3], rhs=ident[:], start=True, stop=True), reads=["xp", "ident"], writes=[pk])
                    kb.op("dve", lambda e: e.tensor_copy(out=convout[:, cc * 128:(cc + 1) * 128], in_=ps[0:3, 0:128]), reads=[pk], writes=["convout"])
            conv_chunk(h, qc, "qc")
            conv_chunk(6 + h, kc_, "kc")

            def vc_cons(ps, ti, ts, pk):
                kb.op("act", lambda e: e.activation(out=vcb[0:ts, ti, :], in_=ps, func=AF.Identity), reads=[pk], writes=["vcb"])
            proj_tm(l, OFF["v_c"] + h * 128, 128, NT, vc_cons)

            def o_cons(ps, t0, n, pk):
                kb.op("act", lambda e: e.activation(out=osig[:, t0:t0 + n], in_=ps, func=AF.Sigmoid), reads=[pk], writes=["osig"])
            proj_fm(l, OFF["o_c"] + h * 128, NT, o_cons)

            def zc_cons(ps, t0, n, pk):
                kb.op("act", lambda e: e.activation(out=Gc[:, t0:t0 + n], in_=ps, func=AF.Silu), reads=[pk], writes=["Gc"])
                kb.op("dve", lambda e: e.tensor_tensor(out=Gc[:, t0:t0 + n], in0=Gc[:, t0:t0 + n], in1=osig[:, t0:t0 + n], op=ALU.mult), reads=["Gc", "osig"], writes=["Gc"])
            proj_fm(l, OFF["z_c"] + h * 128, NT, zc_cons)

            for t0 in range(0, NT, 512):
                n = min(512, NT - t0)
                ps, pk = next_ps()
                kb.op("pe", lambda e: e.matmul(ps[:, 0:n], lhsT=sel6[:, h * 128:(h + 1) * 128], rhs=r_rows[:, t0:t0 + n], start=True, stop=True), reads=["sel6", "r_rows"], writes=[pk])
                kb.op("dve", lambda e: e.tensor_tensor(out=ktil[:, t0:t0 + n], in0=kc_[:, t0:t0 + n], in1=ps[:, 0:n], op=ALU.mult), reads=[pk, "kc"], writes=["ktil"])
                ps, pk = next_ps()
                kb.op("pe", lambda e: e.matmul(ps[:, 0:n], lhsT=sel6[:, h * 128:(h + 1) * 128], rhs=ef_rows[:, t0:t0 + n], start=True, stop=True), reads=["sel6", "ef_rows"], writes=[pk])
                kb.op("act", lambda e: e.activation(out=efb[:, t0:t0 + n], in_=ps[:, 0:n], func=AF.Identity), reads=[pk], writes=["efb"])
            ps, pk = next_ps()
            kb.op("pe", lambda e: e.matmul(ps[:, 0:NTL], lhsT=sel6[:, h * 128:(h + 1) * 128], rhs=wcr[:, 0:NTL], start=True, stop=True), reads=["sel6", "wcr"], writes=[pk])
            kb.op("act", lambda e: e.activation(out=wcb[:, 0:NTL], in_=ps[:, 0:NTL], func=AF.Identity), reads=[pk], writes=["wcb"])

            for t4 in range(0, NTL, 4):
                ps, pk = next_ps()
                nn = min(4, NTL - t4)
                for j in range(nn):
                    ti = t4 + j
                    tsl = slice(ti * TS, (ti + 1) * TS)
                    kb.op("pe", lambda e: e.matmul(ps[0:TS, j * 128:j * 128 + TS], lhsT=ktil[:, tsl], rhs=qc[:, tsl], start=True, stop=True), reads=["ktil", "qc"], writes=[pk])
                for j in range(nn):
                    ti = t4 + j
                    kb.op("dve", lambda e: e.tensor_tensor(out=aT_all[0:TS, ti, 0:TS], in0=ps[0:TS, j * 128:j * 128 + TS], in1=triu[0:TS, 0:TS], op=ALU.mult),
                          reads=[pk, "triu"], writes=["aTa%d" % ti])
            for ti in range(NTL):
                tsl = slice(ti * TS, (ti + 1) * TS)
                kb.op("pe", lambda e: e.transpose(PSB[0:TS, ti * 128:(ti + 1) * 128], ktil[:, tsl], ident_b[:, :]), reads=["ktil", "ident_b"], writes=["psb"])
            kb.op("act", lambda e: e.activation(out=ktm_all[0:TS, 0:NTL, :], in_=PSB[0:TS, 0:NTL * 128].rearrange("p (t f) -> p t f", f=128), func=AF.Identity),
                  reads=["psb"], writes=["ktma"])
            for ti in range(NTL):
                ps3, pk3 = next_ps()
                kb.op("pe", lambda e: e.matmul(ps3[:, 0:128], lhsT=ktm_all[0:TS, ti, :], rhs=vcb[0:TS, ti, :], start=True, stop=True), reads=["ktma", "vcb"], writes=[pk3])
                kb.op("pe", lambda e: e.matmul(ps3[:, 128:129], lhsT=ktm_all[0:TS, ti, :], rhs=ones_b[0:TS, 0:1], start=True, stop=True), reads=["ktma", "ones_b"], writes=[pk3])
                kb.op("act", lambda e: e.activation(out=U_all[:, ti, :], in_=ps3[:, 0:129], func=AF.Identity), reads=[pk3], writes=["Ua%d" % ti])
            for ti in range(NTL):
                kb.op("dve", lambda e: e.tensor_scalar(out=Cst[:, h, :], in0=Cst[:, h, :], scalar1=wcb[:, ti:ti + 1], scalar2=None, op0=ALU.mult), reads=["Cst", "wcb"], writes=["Cst"])
                kb.op("dve", lambda e: e.tensor_scalar(out=nst[:, h:h + 1], in0=nst[:, h:h + 1], scalar1=wcb[:, ti:ti + 1], scalar2=None, op0=ALU.mult), reads=["nst", "wcb"], writes=["nst"])
                kb.op("dve", lambda e: e.tensor_copy(out=Cbf_all[:, ti, :], in_=Cst[:, h, :]), reads=["Cst"], writes=["Cbfa%d" % ti])
                kb.op("dve", lambda e: e.tensor_scalar(out=nrep_all[:, ti, :], in0=ones_f[:], scalar1=nst[:, h:h + 1], scalar2=None, op0=ALU.mult), reads=["nst", "ones_f"], writes=["nrepa%d" % ti])
                kb.op("dve", lambda e: e.tensor_tensor(out=Cst[:, h, :], in0=Cst[:, h, :], in1=U_all[:, ti, 0:128], op=ALU.add), reads=["Ua%d" % ti, "Cst"], writes=["Cst"])
                kb.op("dve", lambda e: e.tensor_tensor(out=nst[:, h:h + 1], in0=nst[:, h:h + 1], in1=U_all[:, ti, 128:129], op=ALU.add), reads=["Ua%d" % ti, "nst"], writes=["nst"])
            for ti in range(NTL):
                tsl = slice(ti * TS, (ti + 1) * TS)
                ps2, pk2 = next_ps()
                kb.op("pe", lambda e: e.matmul(ps2[:, 0:TS], lhsT=vcb[0:TS, ti, :], rhs=aT_all[0:TS, ti, 0:TS], start=True, stop=False), reads=["vcb", "aTa%d" % ti], writes=[pk2])
                kb.op("pe", lambda e: e.matmul(ps2[:, 0:TS], lhsT=Cbf_all[:, ti, :], rhs=qc[:, tsl], start=False, stop=True), reads=["Cbfa%d" % ti, "qc"], writes=[pk2])
                kb.op("pe", lambda e: e.matmul(ps2[:, 128:128 + TS], lhsT=ones_b[0:TS, :], rhs=aT_all[0:TS, ti, 0:TS], start=True, stop=False), reads=["ones_b", "aTa%d" % ti], writes=[pk2])
                kb.op("pe", lambda e: e.matmul(ps2[:, 128:128 + TS], lhsT=nrep_all[:, ti, :], rhs=qc[:, tsl], start=False, stop=True), reads=["nrepa%d" % ti, "qc"], writes=[pk2])
                kb.op("act", lambda e: e.activation(out=dnm[:, 0:TS], in_=ps2[:, 128:128 + TS], func=AF.Abs), reads=[pk2], writes=["dnm"])
                kb.op("dve", lambda e: e.tensor_tensor(out=dnm[:, 0:TS], in0=dnm[:, 0:TS], in1=efb[:, tsl], op=ALU.max), reads=["dnm", "efb"], writes=["dnm"])
                kb.op("dve", lambda e: e.reciprocal(out=dnm[:, 0:TS], in_=dnm[:, 0:TS]), reads=["dnm"], writes=["dnm"])
                kb.op("dve", lambda e: e.tensor_tensor(out=hbuf[:, tsl], in0=ps2[:, 0:TS], in1=dnm[:, 0:TS], op=ALU.mult), reads=[pk2, "dnm"], writes=["hbuf"])
            kb.op("act", lambda e: e.activation(out=sqb[:, 0:NT], in_=hbuf[:, 0:NT], func=AF.Square), reads=["hbuf"], writes=["sqb"])
            for t0 in range(0, NT, 512):
                n = min(512, NT - t0)
                ps, pk = next_ps()
                kb.op("pe", lambda e: e.matmul(ps[:, 0:n], lhsT=o128_b[:], rhs=sqb[:, t0:t0 + n], start=True, stop=True), reads=["sqb", "o128_b"], writes=[pk])
                rsqrt_act(rst[:, t0:t0 + n], ps[:, 0:n], 1.0, [pk], ["rst"])
            kb.op("dve", lambda e: e.scalar_tensor_tensor(out=hbuf[:, 0:NT], in0=hbuf[:, 0:NT], scalar=hngT[:, h:h + 1], in1=rst[:, 0:NT], op0=ALU.mult, op1=ALU.mult),
                  reads=["hbuf", "hngT", "rst"], writes=["hbuf"])
            kb.op("dve", lambda e: e.tensor_tensor(out=yT[:, 10 + h, 0:NT], in0=hbuf[:, 0:NT], in1=Gc[:, 0:NT], op=ALU.mult), reads=["hbuf", "Gc"], writes=["yT"])

        if last:
            oc, on, om, ocv = (s_C, s_n, s_m, s_conv) if sample else (p_C, p_n, p_m, p_conv)
            kb.dma("sp", "st_o", oc[l, :, :, :].rearrange("h k v -> k h v"), Cst[:], reads=["Cst"], writes=["o_C%d" % r])
            with nc.allow_non_contiguous_dma(reason="tiny state stores"):
                kb.dma("sp", "st_o", on[l, :, :].rearrange("h k -> k h"), nst[:], reads=["nst"], writes=["o_n%d" % r])
                kb.dma("sp", "st_o", om[l, :].rearrange("(h o) -> h o", o=1), mall[:, 8:9], reads=["mall"], writes=["o_m%d" % r])
            kb.dma("sp", "st_o", ocv[l, :, :], convout[:], reads=["convout"], writes=["o_cv%d" % r])

        kb.release(m_ph)
        if cfg.stage < 7:
            raise _Stop()
        m_ph = kb.mark()
        WO = [A("wo%d" % i, [128, KC, 512], BF16) for i in range(2)]
        gate_b = A("gate_b", [128, D], F32)
        ostage = [A("ost%d" % i, [128, 512], F32) for i in range(2)]
        xres = [A("xres%d" % i, [128, 512], F32) for i in range(2)]
        kb.dma("sp", "gate_b", gate_b[:], bass.AP(modd.tensor, r * 3 * D + 2 * D, [[0, 128], [1, D]]), reads=["modd"], writes=["gate_b"])
        for cg in range(4):
            wo, wokey = WO[cg % 2], "wo%d" % (cg % 2)
            kb.dma("pool", wokey, wo[:], w_out[l, :, cg * 512:(cg + 1) * 512].rearrange("(kc p) c -> p kc c", p=128), writes=[wokey])
            for ti in range(NTL):
                ps, pk = next_ps()
                for k in range(KC):
                    kb.op("pe", lambda e: e.matmul(ps[0:TS, 0:512], lhsT=yT[:, k, ti * TS:(ti + 1) * TS], rhs=wo[:, k, :], start=(k == 0), stop=(k == KC - 1)),
                          reads=["yT", wokey], writes=[pk])
                i2 = (cg * NTL + ti) % 2
                xr, xrk = xres[i2], "xres%d" % i2
                os_, osk = ostage[i2], "ost%d" % i2
                kb.dma("sp", xrk, xr[0:TS, :], xsrc_all[tok0 + ti * TS:tok0 + (ti + 1) * TS, cg * 512:(cg + 1) * 512], reads=[xkey], writes=[xrk])
                kb.op("dve", lambda e: e.tensor_tensor(out=os_[0:TS, :], in0=ps[0:TS, 0:512], in1=gate_b[0:TS, cg * 512:(cg + 1) * 512], op=ALU.mult),
                      reads=[pk, "gate_b"], writes=[osk])
                kb.op("dve", lambda e: e.tensor_tensor(out=os_[0:TS, :], in0=os_[0:TS, :], in1=xr[0:TS, :], op=ALU.add), reads=[osk, xrk], writes=[osk])
                kb.dma("sp", osk, xdst[tok0 + ti * TS:tok0 + (ti + 1) * TS, cg * 512:(cg + 1) * 512], os_[0:TS, :], reads=[osk], writes=[xkey + "_w%d_%d" % (cg, ti)])
        wl = [xkey + "_w%d_%d" % (cg, ti) for cg in range(4) for ti in range(NTL)]
        kb.op("dve", lambda e: e.memset(stat[:, 7:8], 0.0), reads=wl, writes=[xkey])
        kb.release(m_ph)

    try:
        if cfg.stage < 2:
            raise _Stop()
        import os
        skip = os.environ.get("K_DBG_SKIP", "")
        for l in range(L):
            if "ada" not in skip:
                ada_mod(l)
            if "lc" not in skip:
                load_layer_consts(l)
            if cfg.stage < 3:
                raise _Stop()
            for s in range(NSUP):
                unit(l, s, False)
            unit(l, 0, True)
    except _Stop:
        pass
    kb.finish("sp")
    return nc


_CACHE = {}


def kernel(**inputs):
    cfg = Cfg()
    x_prompt = np.asarray(inputs["x_prompt"], np.float32)
    B, SEQ, _ = x_prompt.shape
    DB, T, _ = inputs["x_sample"].shape
    L = inputs["w_in"].shape[0]
    cfg = Cfg(depth=L, seq=SEQ, nsamp_tok=T, win=inputs["cache_k_win"].shape[2])
    key = (L, SEQ, T, Cfg.stage)
    if key not in _CACHE:
        _CACHE[key] = build_program(cfg)
    nc = _CACHE[key]
    hcst = host_consts()
    f = lambda a: np.ascontiguousarray(np.asarray(a, np.float32))
    shared = {k: f(inputs[k]) for k in ["rel_bias", "norm_g", "ada_w", "ada_b", "w_in", "qn_g", "kn_g", "sgu_g", "sgu_w", "sgu_b",
                                        "conv_w", "conv_b", "f_bias", "i_bias", "hn_g", "w_out"]}
    in_maps = []
    import os
    n = int(os.environ.get('K_NCORES', '8'))
    for c in range(n):
        b = c % B
        m = dict(shared)
        m["x_p"] = f(x_prompt[b])
        m["x_s"] = f(inputs["x_sample"][c])
        m["c_in"] = f(np.stack([inputs["c_prompt"][b], inputs["c_sample"][c]]))
        m["ck"] = f(np.asarray(inputs["cache_k_win"])[:, c].reshape(L, -1, 768))
        m["cv"] = f(np.asarray(inputs["cache_v_win"])[:, c].reshape(L, -1, 768))
        m["st_conv"] = f(np.asarray(inputs["state_conv"])[:, c])
        m["st_C"] = f(np.asarray(inputs["state_C"])[:, c])
        m["st_n"] = f(np.asarray(inputs["state_n"])[:, c])
        m["st_m"] = f(np.asarray(inputs["state_m"])[:, c])
        for k, v in hcst.items():
            m["c_" + k] = v
        in_maps.append(m)
    res = run_bass_kernel_spmd(nc, in_maps, core_ids=list(range(n)))
    R = res.results
    WK = cfg.wkeep
    st = lambda name, cores: np.stack([np.asarray(R[c][name], np.float32) for c in cores])
    pc = list(range(min(B, n)))
    sc = list(range(n))
    y_prompt = st("y_p", pc)
    y_sample = st("y_s", sc)
    p_k = st("p_k", pc).transpose(1, 0, 2, 3).reshape(L, len(pc), WK, 12, 64)
    p_v = st("p_v", pc).transpose(1, 0, 2, 3).reshape(L, len(pc), WK, 12, 64)
    p_conv = st("p_conv", pc).transpose(1, 0, 2, 3)
    p_C = st("p_C", pc).transpose(1, 0, 2, 3, 4)
    p_n = st("p_n", pc).transpose(1, 0, 2, 3)
    p_m = st("p_m", pc).transpose(1, 0, 2)
    s_k = st("s_k", sc).transpose(1, 0, 2, 3).reshape(L, n, T, 12, 64)
    s_v = st("s_v", sc).transpose(1, 0, 2, 3).reshape(L, n, T, 12, 64)
    s_sgu = st("s_sgu", sc).transpose(1, 0, 2, 3)
    s_conv = st("s_conv", sc).transpose(1, 0, 2, 3)
    s_C = st("s_C", sc).transpose(1, 0, 2, 3, 4)
    s_n = st("s_n", sc).transpose(1, 0, 2, 3)
    s_m = st("s_m", sc).transpose(1, 0, 2)
    outs = (y_prompt, y_sample, p_k, p_v, p_conv, p_C, p_n, p_m, s_k, s_v, s_sgu, s_conv, s_C, s_n, s_m)
    return tuple(np.ascontiguousarray(o, dtype=np.float32) for o in outs)
```
